# Optimizing a Trainium2 kernel written in Bass

```python
import math
import jax, jax.numpy as jnp
from jax import lax
import numpy as np

D_MODEL = 2048
BATCH = 2
SEQ = 4096
DEPTH = 1

HEAD_DIM = 128
N_HEADS = D_MODEL // HEAD_DIM
N_MOBA_HEADS = N_HEADS // 2
N_FOX_HEADS = N_HEADS - N_MOBA_HEADS
W_MOBA = N_MOBA_HEADS * HEAD_DIM
W_FOX = N_FOX_HEADS * HEAD_DIM
IN_COLS = 3 * W_MOBA + 3 * W_FOX + N_FOX_HEADS
MOBA_BLOCK = 256
MOBA_TOPK = 3
MOBA_Q_CHUNK = 32
FOX_Q_BLOCK = 128
FOX_GATE_BIAS = 2.0
ROPE_THETA = 10000.0
N_MEM = 256
N_XATTN_HEADS = 4
XATTN_DIM = N_XATTN_HEADS * HEAD_DIM
D_FF = 5632
CONV_WIDTH = 3
RMS_EPS = 1e-6
NEG = -1e30
SCALE = 1.0 / math.sqrt(HEAD_DIM)

kernel_name = "hymba_moba_fox_convffn_xattn"


def rmsnorm(x, g):
    xf = x.astype(jnp.float32)
    y = xf * lax.rsqrt(jnp.mean(xf * xf, axis=-1, keepdims=True) + RMS_EPS)
    return (y * g.astype(jnp.float32)).astype(x.dtype)


def split_heads(x, n_heads):
    b, t, _ = x.shape
    return x.reshape(b, t, n_heads, HEAD_DIM).transpose(0, 2, 1, 3)


def merge_heads(x):
    b, h, t, d = x.shape
    return x.transpose(0, 2, 1, 3).reshape(b, t, h * d)


def rope(x, pos):
    half = HEAD_DIM // 2
    inv = ROPE_THETA ** (-jnp.arange(half, dtype=jnp.float32) / half)
    ang = pos.astype(jnp.float32)[:, None] * inv[None, :]
    cos, sin = jnp.cos(ang), jnp.sin(ang)
    xf = x.astype(jnp.float32)
    x1, x2 = xf[..., :half], xf[..., half:]
    out = jnp.concatenate([x1 * cos - x2 * sin, x2 * cos + x1 * sin], axis=-1)
    return out.astype(x.dtype)


def moba_attention(q, k, v):
    b, h, t, hd = q.shape
    L = MOBA_BLOCK
    nb = -(-t // L)
    tp = nb * L
    pad = ((0, 0), (0, 0), (0, tp - t), (0, 0))
    q, k, v = jnp.pad(q, pad), jnp.pad(k, pad), jnp.pad(v, pad)
    kb = k.reshape(b, h, nb, L, hd)
    vb = v.reshape(b, h, nb, L, hd)
    kmean = jnp.mean(kb.astype(jnp.float32), axis=3)
    gate = jnp.einsum('bhtd,bhnd->bhtn', q.astype(jnp.float32), kmean)
    qblk = jnp.arange(tp) // L
    past = jnp.arange(nb)[None, :] < qblk[:, None]
    gate = jnp.where(past[None, None], gate, NEG)
    k_sel = min(MOBA_TOPK, nb)
    _, top_idx = lax.top_k(gate, k_sel)
    sel_ok = top_idx < qblk[None, None, :, None]
    C = MOBA_Q_CHUNK
    nc = tp // C
    qc = jnp.moveaxis(q.reshape(b, h, nc, C, hd), 2, 0)
    idxc = jnp.moveaxis(top_idx.reshape(b, h, nc, C, k_sel), 2, 0)
    okc = jnp.moveaxis(sel_ok.reshape(b, h, nc, C, k_sel), 2, 0)
    bi = jnp.arange(b)[:, None, None, None]
    hi = jnp.arange(h)[None, :, None, None]

    def chunk(args):
        ci, qi, idx, ok = args
        start = ci * C
        blk = start // L
        ks = kb[bi, hi, idx]
        vs = vb[bi, hi, idx]
        s_sel = jnp.einsum('bhcd,bhcnld->bhcnl', qi, ks).astype(jnp.float32) * SCALE
        s_sel = jnp.where(ok[..., None], s_sel, NEG).reshape(b, h, C, k_sel * L)
        k_own = lax.dynamic_index_in_dim(kb, blk, axis=2, keepdims=False)
        v_own = lax.dynamic_index_in_dim(vb, blk, axis=2, keepdims=False)
        s_own = jnp.einsum('bhcd,bhld->bhcl', qi, k_own).astype(jnp.float32) * SCALE
        qpos = start + jnp.arange(C)
        kpos = blk * L + jnp.arange(L)
        s_own = jnp.where((kpos[None, :] <= qpos[:, None])[None, None], s_own, NEG)
        p = jax.nn.softmax(jnp.concatenate([s_sel, s_own], axis=-1), axis=-1)
        p_sel = p[..., :k_sel * L].reshape(b, h, C, k_sel, L).astype(v.dtype)
        p_own = p[..., k_sel * L:].astype(v.dtype)
        return (jnp.einsum('bhcnl,bhcnld->bhcd', p_sel, vs)
                + jnp.einsum('bhcl,bhld->bhcd', p_own, v_own))

    out = lax.map(chunk, (jnp.arange(nc), qc, idxc, okc))
    out = jnp.moveaxis(out, 0, 2).reshape(b, h, tp, hd)
    return out[:, :, :t]


def forgetting_attention(q, k, v, log_f):
    b, h, t, hd = q.shape
    c = jnp.cumsum(log_f, axis=-1)
    QB = FOX_Q_BLOCK
    nq = t // QB
    qb = jnp.moveaxis(q.reshape(b, h, nq, QB, hd), 2, 0)
    cb = jnp.moveaxis(c.reshape(b, h, nq, QB), 2, 0)
    kpos = jnp.arange(t)

    def block(args):
        i, qi, ci = args
        s = jnp.einsum('bhqd,bhkd->bhqk', qi, k).astype(jnp.float32) * SCALE
        s = s + ci[..., None] - c[:, :, None, :]
        qpos = i * QB + jnp.arange(QB)
        s = jnp.where((kpos[None, :] <= qpos[:, None])[None, None], s, NEG)
        p = jax.nn.softmax(s, axis=-1).astype(v.dtype)
        return jnp.einsum('bhqk,bhkd->bhqd', p, v)

    out = lax.map(block, (jnp.arange(nq), qb, cb))
    return jnp.moveaxis(out, 0, 2).reshape(b, h, t, hd)


def cross_attention(xn, memn, w_cq, w_ckv, w_co):
    q = split_heads(xn @ w_cq, N_XATTN_HEADS)
    k, v = jnp.split(memn @ w_ckv, 2, axis=-1)
    k = split_heads(k, N_XATTN_HEADS)
    v = split_heads(v, N_XATTN_HEADS)
    s = jnp.einsum('bhtd,bhmd->bhtm', q, k).astype(jnp.float32) * SCALE
    p = jax.nn.softmax(s, axis=-1).astype(v.dtype)
    o = jnp.einsum('bhtm,bhmd->bhtd', p, v)
    return merge_heads(o) @ w_co


def conv_ffn(xn, w_up, conv_w, conv_b, w_down):
    hdn = xn @ w_up
    ch = hdn.shape[-1]
    hdn = lax.conv_general_dilated(
        hdn, conv_w[:, None, :].astype(hdn.dtype), window_strides=(1,),
        padding=[(CONV_WIDTH - 1, 0)], dimension_numbers=('NWC', 'WIO', 'NWC'),
        feature_group_count=ch) + conv_b
    g, u = jnp.split(hdn, 2, axis=-1)
    return (jax.nn.silu(g) * u) @ w_down


def setup_inputs(seed: int = 0) -> dict:
    key = jax.random.key(seed)
    ks = jax.random.split(key, 20)
    f32 = jnp.float32

    def nrm(k, shape, fan_in):
        return jax.random.normal(k, shape, f32) * (fan_in ** -0.5)

    def gain(k, shape):
        return 1.0 + 0.02 * jax.random.normal(k, shape, f32)

    return {
        "x": jax.random.normal(ks[0], (BATCH, SEQ, D_MODEL), f32),
        "mem": jax.random.normal(ks[1], (BATCH, N_MEM, D_MODEL), f32),
        "attn_norm_g": gain(ks[2], (DEPTH, D_MODEL)),
        "w_in": nrm(ks[3], (DEPTH, D_MODEL, IN_COLS), D_MODEL),
        "b_f": FOX_GATE_BIAS + 0.1 * jax.random.normal(ks[4], (DEPTH, N_FOX_HEADS), f32),
        "w_o": nrm(ks[5], (DEPTH, W_MOBA + W_FOX, D_MODEL), W_MOBA + W_FOX),
        "xattn_norm_g": gain(ks[6], (DEPTH, D_MODEL)),
        "mem_norm_g": gain(ks[7], (DEPTH, D_MODEL)),
        "w_cq": nrm(ks[8], (DEPTH, D_MODEL, XATTN_DIM), D_MODEL),
        "w_ckv": nrm(ks[9], (DEPTH, D_MODEL, 2 * XATTN_DIM), D_MODEL),
        "w_co": nrm(ks[10], (DEPTH, XATTN_DIM, D_MODEL), XATTN_DIM),
        "ffn_norm_g": gain(ks[11], (DEPTH, D_MODEL)),
        "w_up": nrm(ks[12], (DEPTH, D_MODEL, 2 * D_FF), D_MODEL),
        "conv_w": nrm(ks[13], (DEPTH, CONV_WIDTH, 2 * D_FF), CONV_WIDTH),
        "conv_b": 0.02 * jax.random.normal(ks[14], (DEPTH, 2 * D_FF), f32),
        "w_down": nrm(ks[15], (DEPTH, D_FF, D_MODEL), D_FF),
        "final_norm_g": gain(ks[16], (D_MODEL,)),
    }


def reference(x, mem, attn_norm_g, w_in, b_f, w_o, xattn_norm_g, mem_norm_g,
              w_cq, w_ckv, w_co, ffn_norm_g, w_up, conv_w, conv_b, w_down,
              final_norm_g):
    t = x.shape[1]
    pos = jnp.arange(t)
    splits = [W_MOBA, 2 * W_MOBA, 3 * W_MOBA,
              3 * W_MOBA + W_FOX, 3 * W_MOBA + 2 * W_FOX, 3 * W_MOBA + 3 * W_FOX]
    h = x
    for l in range(DEPTH):
        xn = rmsnorm(h, attn_norm_g[l])
        proj = xn @ w_in[l]
        qa, ka, va, qf, kf, vf, zf = jnp.split(proj, splits, axis=-1)
        qa = rope(split_heads(qa, N_MOBA_HEADS), pos)
        ka = rope(split_heads(ka, N_MOBA_HEADS), pos)
        va = split_heads(va, N_MOBA_HEADS)
        o_moba = moba_attention(qa, ka, va)
        log_f = jax.nn.log_sigmoid(zf.astype(jnp.float32) + b_f[l].astype(jnp.float32))
        o_fox = forgetting_attention(split_heads(qf, N_FOX_HEADS),
                                     split_heads(kf, N_FOX_HEADS),
                                     split_heads(vf, N_FOX_HEADS),
                                     jnp.transpose(log_f, (0, 2, 1)))
        mixed = jnp.concatenate([merge_heads(o_moba), merge_heads(o_fox)], axis=-1)
        h = h + mixed @ w_o[l]
        h = h + cross_attention(rmsnorm(h, xattn_norm_g[l]), rmsnorm(mem, mem_norm_g[l]),
                                w_cq[l], w_ckv[l], w_co[l])
        h = h + conv_ffn(rmsnorm(h, ffn_norm_g[l]), w_up[l], conv_w[l], conv_b[l], w_down[l])
    return rmsnorm(h, final_norm_g)
```

```python
import os
import numpy as np
import ml_dtypes
import concourse.bass as bass
import concourse.mybir as mybir
from concourse.bass_utils import run_bass_kernel_spmd
from contextlib import ExitStack

F32 = mybir.dt.float32
BF16 = mybir.dt.bfloat16
AF = mybir.ActivationFunctionType
ALU = mybir.AluOpType
AX = mybir.AxisListType
NPBF = ml_dtypes.bfloat16

D = 2048
KC = 16
NCT = 34
NCTX = NCT * 128
QT0 = 25
NQS = 9
NQ = NQS * 128
QW = 384
NBLK = 17
SCALE = 1.0 / float(np.sqrt(128.0))
NEGM = -30000.0
DFF = 5632
NFC = 44
DEBUG = bool(int(os.environ.get("MK_DEBUG", "0")))
STOP_AFTER = os.environ.get("MK_STOP", "")


class Tracker:
    def __init__(self, nc, n_dma_sems=28):
        self.nc = nc
        self.eng = {'pe': nc.tensor, 'act': nc.scalar, 'dve': nc.vector, 'pool': nc.gpsimd, 'sp': nc.sync}
        self.sem = {k: nc.alloc_semaphore('sem_' + k) for k in ['pe', 'act', 'dve', 'pool']}
        self.cnt = {k: 0 for k in self.sem}
        self.dma_sems = [nc.alloc_semaphore('dsem%d' % i) for i in range(n_dma_sems)]
        self.dma_cnt = [0] * n_dma_sems
        self.dma_rr = 0
        self.known = {}
        self.lastw = {}
        self.readers = {}
        self.semobj = dict(self.sem)
        for i, s in enumerate(self.dma_sems):
            self.semobj[('d', i)] = s
        self.n_ops = 0

    def _wait(self, e, semkey, val):
        if semkey == 'pe' and e == 'pe':
            return
        kk = (e, semkey)
        if self.known.get(kk, 0) >= val:
            return
        self.known[kk] = val
        self.eng[e].wait_ge(self.semobj[semkey], val)

    def _deps(self, e, reads, writes):
        for r in reads:
            w = self.lastw.get(r)
            if w is not None:
                self._wait(e, *w)
        for r in writes:
            w = self.lastw.get(r)
            if w is not None:
                self._wait(e, *w)
            rd = self.readers.get(r)
            if rd:
                for sk, v in rd.items():
                    self._wait(e, sk, v)

    def _record(self, tok, reads, writes):
        sk, v = tok
        for r in reads:
            d = self.readers.setdefault(r, {})
            if d.get(sk, 0) < v:
                d[sk] = v
        for r in writes:
            self.lastw[r] = tok
            self.readers[r] = {}

    def op(self, e, fn, reads=(), writes=()):
        self._deps(e, reads, writes)
        ins = fn(self.eng[e])
        self.cnt[e] += 1
        ins.then_inc(self.sem[e], 1)
        tok = (e, self.cnt[e])
        self._record(tok, reads, writes)
        self.n_ops += 1
        return tok

    def dma(self, out, in_, reads=(), writes=(), q='sp'):
        self._deps(q, reads, writes)
        i = self.dma_rr
        self.dma_rr = (self.dma_rr + 1) % len(self.dma_sems)
        if self.dma_cnt[i] > 0:
            self._wait(q, ('d', i), self.dma_cnt[i])
        self.dma_cnt[i] += 16
        self.eng[q].dma_start(out=out, in_=in_).then_inc(self.dma_sems[i], 16)
        tok = (('d', i), self.dma_cnt[i])
        self._record(tok, reads, writes)
        self.n_ops += 1
        return tok

    def barrier(self, engines=('pe', 'act', 'dve', 'pool', 'sp')):
        toks = [(k, c) for k, c in self.cnt.items() if c > 0]
        toks += [(('d', i), c) for i, c in enumerate(self.dma_cnt) if c > 0]
        for e in engines:
            for t in toks:
                if t[0] == e:
                    continue
                self._wait(e, *t)
        self.lastw = {}
        self.readers = {}


def build_program():
    nc = bass.Bass("TRN2", target_bir_lowering=False, dynamic_dma_scratch_size=512)
    T = Tracker(nc)

    def din(name, shape, dt=F32):
        return nc.dram_tensor(name, list(shape), dt, kind="ExternalInput").ap()

    xc = din("xc", [NCTX, D])
    memx = din("memx", [256, D])
    g_attn = din("g_attn", [1, D]); g_x = din("g_x", [1, D]); g_mem = din("g_mem", [1, D])
    g_ffn = din("g_ffn", [1, D]); g_fin = din("g_fin", [1, D])
    w_in = din("w_in", [D, 6152]); w_o = din("w_o", [D, D]); w_cq = din("w_cq", [D, 512])
    w_ckv = din("w_ckv", [D, 1024]); w_co = din("w_co", [512, D]); w_up = din("w_up", [D, 2 * DFF])
    w_down = din("w_down", [DFF, D])
    convw_d = din("convw", [128, 88 * 3]); convb_d = din("convb", [128, 88]); bf_d = din("bf", [8, 1])
    cos_d = din("cosT", [128, NCTX]); sin_d = din("sinT", [128, NCTX])
    kbias_d = din("kbias", [128, NCT]); lfv_d = din("lfvalid", [8, NCT]); halov_d = din("halov", [128, 1])
    pastb_d = din("pastbias", [128, NQS * NBLK]); past01_d = din("past01", [128, NQS * NBLK]); own01_d = din("own01", [128, NQS * NBLK])
    identb_d = din("identb", [128, 128], BF16); rperm_d = din("rperm", [128, 128], BF16)
    onesb_d = din("onesb", [128, 128], BF16); tri_d = din("tri", [128, 128], BF16)
    esel_d = din("esel", [128, NBLK * 128], BF16); ident8_d = din("ident8", [8, 8]); ones8_d = din("ones8", [8, 128])
    y = nc.dram_tensor("y", [1024, D], F32, kind="ExternalOutput").ap()
    mixT_d = nc.dram_tensor("mixT_d", [16, 128, NQ], BF16, kind="Internal").ap()
    hres_d = nc.dram_tensor("hres_d", [NQS, 128, D], F32, kind="Internal").ap()
    dbg = {}
    if DEBUG:
        dbg['xnT'] = nc.dram_tensor("dbg_xnT", [128, KC, NCTX], BF16, kind="ExternalOutput").ap()
        dbg['ck'] = nc.dram_tensor("dbg_ck", [128, NCT * 8], F32, kind="ExternalOutput").ap()
        dbg['kT'] = nc.dram_tensor("dbg_kT", [16, 128, NCTX], BF16, kind="ExternalOutput").ap()
        dbg['V'] = nc.dram_tensor("dbg_V", [16, 128, NCT * 128], BF16, kind="ExternalOutput").ap()
        dbg['qT'] = nc.dram_tensor("dbg_qT", [16, 128, NQ], BF16, kind="ExternalOutput").ap()
        dbg['mixT'] = nc.dram_tensor("dbg_mixT", [16, 128, NQ], BF16, kind="ExternalOutput").ap()
        dbg['h1'] = nc.dram_tensor("dbg_h1", [NQS, 128, D], F32, kind="ExternalOutput").ap()
        dbg['h2'] = nc.dram_tensor("dbg_h2", [NQS, 128, D], F32, kind="ExternalOutput").ap()
        dbg['h3'] = nc.dram_tensor("dbg_h3", [NQS, 128, D], F32, kind="ExternalOutput").ap()

    wv = lambda ap: ap.rearrange("(c p) n -> p c n", p=128)

    psA = [nc.alloc_psum_tensor("psA%d" % i, [128, 512], F32) for i in range(2)]
    psS = [nc.alloc_psum_tensor("psS%d" % i, [128, 512], F32) for i in range(2)]
    psTf = [nc.alloc_psum_tensor("psT%d" % i, [128, 512], F32) for i in range(2)]

    class _BV:
        def __init__(self, t):
            self.t = t

        def __getitem__(self, key):
            return self.t[:].bitcast(BF16)[key]
    psT = [_BV(t) for t in psTf]
    psO = nc.alloc_psum_tensor("psO", [128, 512], F32)
    psD = nc.alloc_psum_tensor("psD", [128, 512], F32)

    A = lambda name, shape, dt: nc.alloc_sbuf_tensor("sb_" + name, shape, dt)
    identb = A("identb", [128, 128], BF16); rperm = A("rperm", [128, 128], BF16)
    onesb = A("onesb", [128, 128], BF16); tri = A("tri", [128, 128], BF16)
    ident8 = A("ident8", [8, 8], F32); ones8 = A("ones8", [8, 128], F32)
    halov = A("halov", [128, 1], F32)
    for t_, d_, nm in [(identb, identb_d, 'identb'), (rperm, rperm_d, 'rperm'), (onesb, onesb_d, 'onesb'), (tri, tri_d, 'tri'),
                       (ident8, ident8_d, 'ident8'), (ones8, ones8_d, 'ones8'), (halov, halov_d, 'halov')]:
        T.dma(t_[:], d_, writes=[nm])
    wst = [A("wst%d" % i, [128, 2048], F32) for i in range(2)]
    wst_i = [0]
    stat = A("stat", [128, 2 * (NCT + 2 * NQS + 12)], F32)
    stat_i = [0]

    def load_w(dst, dkey, src, G, C, engs=('pool',)):
        gs = max(1, 2048 // C)
        g0 = 0
        k = 0
        while g0 < G:
            g1 = min(G, g0 + gs)
            i = wst_i[0] % 2
            wst_i[0] += 1
            st = wst[i][:, 0:(g1 - g0) * C].rearrange("p (g c) -> p g c", c=C)
            T.dma(st, src[:, g0:g1, :], writes=[('wst', i)])
            en = engs[k % len(engs)]
            k += 1
            if en == 'act':
                T.op('act', lambda e: e.activation(out=dst[:, g0:g1, :], in_=st, func=AF.Copy), reads=[('wst', i)], writes=[dkey])
            else:
                T.op(en, lambda e: e.tensor_copy(out=dst[:, g0:g1, :], in_=st), reads=[('wst', i)], writes=[dkey])
            g0 = g1

    def rms_sq(src, skey):
        k = stat_i[0]
        stat_i[0] += 1
        ss = stat[:, 2 * k:2 * k + 1]
        rs = stat[:, 2 * k + 1:2 * k + 2]
        sk = ('stat', k)
        T.op('act', lambda e: e.activation(out=junk[:], in_=src, func=AF.Square, accum_out=ss), reads=[skey], writes=['junk', sk])
        return ss, rs, sk

    def rms_fin(ss, rs, sk):
        T.op('dve', lambda e: e.tensor_scalar(out=rs, in0=ss, scalar1=1.0 / D, scalar2=1e-6, op0=ALU.mult, op1=ALU.add), reads=[sk], writes=[sk])
        T.op('act', lambda e: e.activation(out=rs, in_=rs, func=AF.Sqrt), reads=[sk], writes=[sk])
        T.op('dve', lambda e: e.reciprocal(out=rs, in_=rs), reads=[sk], writes=[sk])

    def rms_stats(src, skey):
        ss, rs, sk = rms_sq(src, skey)
        rms_fin(ss, rs, sk)
        return rs, sk

    tr_i = [0]

    def norm_mul(src, skey, rs, sk):
        b = tr_i[0] % 2
        xb = xnb[b]
        T.op('dve', lambda e: e.scalar_tensor_tensor(out=xb[:], in0=src, scalar=rs, in1=grep[:], op0=ALU.mult, op1=ALU.mult),
             reads=[skey, sk, 'grep'], writes=[('xnb', b)])
        return b

    def norm_tr(b, dstT, dkey, col):
        xb = xnb[b]
        for half in range(2):
            pb = tr_i[0] % 2
            tr_i[0] += 1
            pt = psT[pb]
            for k in range(8):
                kc = half * 8 + k
                T.op('pe', lambda e: e.transpose(out=pt[:, k * 128:(k + 1) * 128], in_=xb[:, kc * 128:(kc + 1) * 128], identity=identb[:]),
                     reads=[('xnb', b), 'identb'], writes=[('psT', pb)])
            src_v = pt[:, :].rearrange("p (a b) -> p a b", b=128)
            dst_v = dstT[:, half * 8:(half + 1) * 8, col:col + 128]
            T.op('act', lambda e: e.activation(out=dst_v, in_=src_v, func=AF.Copy), reads=[('psT', pb)], writes=[dkey])

    def norm_seq(items, dstT):
        pend = None
        for (load_fn, src, skey, dkey, col) in items:
            if pend is not None:
                b = norm_mul(pend[0], pend[1], pend[2], pend[3])
            if load_fn is not None:
                load_fn()
            ss, rs, sk = rms_sq(src, skey)
            if pend is not None:
                norm_tr(b, dstT, pend[4], pend[5])
            rms_fin(ss, rs, sk)
            pend = (src, skey, rs, sk, dkey, col)
        b = norm_mul(pend[0], pend[1], pend[2], pend[3])
        norm_tr(b, dstT, pend[4], pend[5])

    class NormPipe:
        def __init__(self, dstT):
            self.dstT = dstT
            self.statted = None
            self.mulled = None

        def _step(self, new):
            m = self.mulled
            if m is not None:
                norm_tr(m[0], self.dstT, m[1], m[2])
                if m[3] is not None:
                    m[3]()
                self.mulled = None
            p = self.statted
            if p is not None:
                b = norm_mul(p[0], p[1], p[2], p[3])
                self.mulled = (b, p[4], p[5], p[6])
                self.statted = None
            if new is not None:
                src, skey, dkey, col, after = new
                ss, rs, sk = rms_sq(src, skey)
                rms_fin(ss, rs, sk)
                self.statted = (src, skey, rs, sk, dkey, col, after)

        def push(self, src, skey, dkey, col, after=None):
            self._step((src, skey, dkey, col, after))

        def flush(self):
            self._step(None)
            self._step(None)

    def load_g(g_d):
        T.dma(grep[:], g_d.partition_broadcast(128), writes=['grep'])

    es_top = ExitStack()
    _uid = [0]

    def S(st, name, shape, dt, side=None):
        _uid[0] += 1
        return st.enter_context(nc.sbuf_tensor("sb%d_%s" % (_uid[0], name), list(shape), dt, side=side))

    es_A = ExitStack()
    xnT = S(es_A, "xnT", [128, KC, NCTX], BF16)
    ck = S(es_A, "ck", [128, NCT, 8], F32)
    ckk = S(es_A, "ckk", [128, NCT, 8], F32)
    cref = S(es_A, "cref", [128, 24], F32)
    kbias = S(es_A, "kbias", [128, NCT], F32)
    wzf = S(es_A, "wzf", [128, KC, 8], F32)
    wzb = S(es_A, "wzb", [128, KC, 8], BF16)
    bfn = S(es_A, "bfn", [8, 1], F32)
    lfv = S(es_A, "lfv", [8, NCT], F32)
    one8 = S(es_A, "one8", [8, 1], F32)
    rhsd = S(es_A, "rhsd", [8, 8, 3], F32)
    T.dma(kbias[:], kbias_d, writes=['kbias'])

    es = ExitStack()
    xt = [S(es, "xt%d" % i, [128, D], F32) for i in range(3)]
    junk = S(es, "junk", [128, D], BF16)
    xnb = [S(es, "xnb%d" % i, [128, D], BF16) for i in range(2)]
    grep = S(es, "grep", [128, D], F32)
    load_g(g_attn)
    T.dma(wzf[:], wv(w_in)[:, :, 6144:6152], writes=['wzf'])
    T.dma(bfn[:], bf_d, writes=['bfn'])
    T.dma(lfv[:], lfv_d, writes=['lfv'])
    npipe0 = NormPipe(xnT)
    for tl in range(NCT):
        b = tl % 3
        T.dma(xt[b][:], xc[tl * 128:(tl + 1) * 128, :], writes=[('xt', b)])
        npipe0.push(xt[b][:], ('xt', b), ('xnT', tl), tl * 128)
    npipe0.flush()
    if DEBUG:
        T.dma(dbg['xnT'], xnT[:], reads=[('xnT', tl) for tl in range(NCT)])

    T.barrier()
    es.close()
    es = ExitStack()
    zl = S(es, "zl", [8, NCTX], F32)
    cT = S(es, "cT", [8, NCTX], F32)
    T.op('pool', lambda e: e.tensor_copy(out=wzb[:], in_=wzf[:]), reads=['wzf'], writes=['wzb'])
    T.op('dve', lambda e: e.tensor_scalar(out=bfn[:], in0=bfn[:], scalar1=-1.0, scalar2=None, op0=ALU.mult), reads=['bfn'], writes=['bfn'])
    T.op('dve', lambda e: e.memset(one8[:], 1.0), writes=['one8'])
    groups = [(g * 512, 512) for g in range(8)] + [(4096, 256)]
    for gi, (c0, n) in enumerate(groups):
        pa = psA[gi % 2]
        for kc in range(KC):
            T.op('pe', lambda e: e.matmul(pa[0:8, 0:n], lhsT=wzb[:, kc, :], rhs=xnT[:, kc, c0:c0 + n], start=(kc == 0), stop=(kc == KC - 1)),
                 reads=['wzb'] + [('xnT', tl) for tl in range(c0 // 128, (c0 + n) // 128)], writes=[('psA', gi % 2)])
        T.op('act', lambda e: e.activation(out=zl[:, c0:c0 + n], in_=pa[0:8, 0:n], func=AF.Exp, scale=-1.0, bias=bfn[:]),
             reads=[('psA', gi % 2), 'bfn'], writes=['zl'])
    T.op('act', lambda e: e.activation(out=zl[:], in_=zl[:], func=AF.Ln, bias=1.0), reads=['zl'], writes=['zl'])
    T.op('dve', lambda e: e.scalar_tensor_tensor(out=zl[:].rearrange("p (a b) -> p a b", b=128), in0=zl[:].rearrange("p (a b) -> p a b", b=128),
                                                 scalar=-1.0, in1=lfv[:].unsqueeze(2).to_broadcast([8, NCT, 128]), op0=ALU.mult, op1=ALU.mult),
         reads=['zl', 'lfv'], writes=['zl'])
    T.op('dve', lambda e: e.tensor_tensor_scan(out=cT[:], data0=one8[:].to_broadcast([8, NCTX]), data1=zl[:], initial=0.0, op0=ALU.mult, op1=ALU.add),
         reads=['zl', 'one8'], writes=['cT'])
    for tl in range(NCT):
        T.op('pe', lambda e: e.transpose(out=psS[0][:, tl * 8:(tl + 1) * 8], in_=cT[:, tl * 128:(tl + 1) * 128], identity=ident8[:]),
             reads=['cT', 'ident8'], writes=[('psS', 0)])
    T.op('dve', lambda e: e.tensor_copy(out=ck[:], in_=psS[0][:, 0:NCT * 8].rearrange("p (a b) -> p a b", b=8)), reads=[('psS', 0)], writes=['ck'])
    T.op('dve', lambda e: e.tensor_tensor(out=ckk[:], in0=ck[:], in1=kbias[:].unsqueeze(2).to_broadcast([128, NCT, 8]), op=ALU.subtract),
         reads=['ck', 'kbias'], writes=['ckk'])
    for qt in range(3):
        tm = QT0 * 128 + qt * QW + QW // 2
        T.op('dve', lambda e: e.tensor_scalar(out=rhsd[:, :, qt], in0=ident8[:], scalar1=cT[:, tm:tm + 1], scalar2=None, op0=ALU.mult),
             reads=['cT', 'ident8'], writes=['rhsd'])
    T.op('pe', lambda e: e.matmul(psS[1][:, 0:24], lhsT=ones8[:], rhs=rhsd[:].rearrange("p a b -> p (a b)"), start=True, stop=True),
         reads=['ones8', 'rhsd'], writes=[('psS', 1)])
    T.op('dve', lambda e: e.tensor_copy(out=cref[:], in_=psS[1][:, 0:24]), reads=[('psS', 1)], writes=['cref'])
    if DEBUG:
        T.dma(dbg['ck'], ck[:].rearrange("p a b -> p (a b)"), reads=['ck'])
    T.barrier()
    es.close()

    es = ExitStack()
    wq = S(es, "wq", [128, KC, 128], BF16); wk = S(es, "wk", [128, KC, 128], BF16); wvv = S(es, "wvv", [128, KC, 128], BF16)
    kT = S(es, "kT", [128, NCTX], BF16)
    V = S(es, "V", [128, NCT, 128], BF16)
    qT = S(es, "qT", [128, NQ], BF16)
    vtmp = [S(es, "vtmp%d" % i, [128, 512], BF16) for i in range(2)]
    kbf = [S(es, "kbf%d" % i, [128, 512], BF16) for i in range(2)]
    cs = [S(es, "cs%d" % i, [128, 512], F32) for i in range(2)]
    sn = [S(es, "sn%d" % i, [128, 512], F32) for i in range(2)]
    Pt = [S(es, "Pt%d" % i, [128, QW], BF16) for i in range(3)]
    maskT = S(es, "maskT", [128, NQ], BF16)
    T.op('pool', lambda e: e.memset(maskT[:], 0.0), writes=[('maskT', 0), ('maskT', 1), ('maskT', 2)])
    fb = S(es, "fb", [128, 3, NCT], F32)
    rec = [S(es, "rec%d" % i, [128, QW], F32) for i in range(2)]
    mixo = [S(es, "mixo%d" % i, [128, QW], BF16) for i in range(2)]
    esel = S(es, "esel", [128, NBLK * 128], BF16)
    pastb = S(es, "pastb", [128, NQS * NBLK], F32); past01 = S(es, "past01", [128, NQS * NBLK], F32); own01 = S(es, "own01", [128, NQS * NBLK], F32)
    ksum = S(es, "ksum", [128, NBLK], F32); ksumb = S(es, "ksumb", [128, NBLK], BF16)
    gm = S(es, "gm", [128, NQS * NBLK], F32); m8 = S(es, "m8", [128, NQS * 8], F32); ge = S(es, "ge", [128, NQS * NBLK], F32)
    maskv = S(es, "maskv", [128, NQS, 32], BF16)
    T.op('dve', lambda e: e.memset(maskv[:], 0.0), writes=['maskv'])
    for t_, d_, nm in [(esel, esel_d, 'esel'), (pastb, pastb_d, 'pastb'), (past01, past01_d, 'past01'), (own01, own01_d, 'own01')]:
        T.dma(t_[:], d_, writes=[nm])

    rope_i = [0]
    pa_i = [0]

    def proj_mm(wt, wkey, c0, n):
        pi = pa_i[0] % 2
        pa_i[0] += 1
        pa = psA[pi]
        xkeys = [('xnT', tl) for tl in range(c0 // 128, (c0 + n) // 128)]
        for kc in range(KC):
            T.op('pe', lambda e: e.matmul(pa[:, 0:n], lhsT=wt[:, kc, :], rhs=xnT[:, kc, c0:c0 + n], start=(kc == 0), stop=(kc == KC - 1)),
                 reads=[wkey] + xkeys, writes=[('psA', pi)])
        return pi

    def proj_fin(pi, c0, n, moba, dst, dkey, dcol):
        pa = psA[pi]
        if not moba:
            T.op('act', lambda e: e.activation(out=dst[:, dcol:dcol + n], in_=pa[:, 0:n], func=AF.Copy), reads=[('psA', pi)], writes=[dkey])
            return
        r = rope_i[0] % 2
        rope_i[0] += 1
        T.dma(cs[r][:, 0:n], cos_d[:, c0:c0 + n], writes=[('cs', r)])
        T.dma(sn[r][:, 0:n], sin_d[:, c0:c0 + n], writes=[('sn', r)])
        T.op('act', lambda e: e.activation(out=kbf[r][:, 0:n], in_=pa[:, 0:n], func=AF.Copy), reads=[('psA', pi)], writes=[('kbf', r)])
        return r

    def rope_fin(r, n, dst, dkey, dcol):
        T.op('pe', lambda e: e.matmul(psS[r][:, 0:n], lhsT=rperm[:], rhs=kbf[r][:, 0:n], start=True, stop=True),
             reads=['rperm', ('kbf', r)], writes=[('psS', r)])
        T.op('pool', lambda e: e.tensor_tensor(out=cs[r][:, 0:n], in0=kbf[r][:, 0:n], in1=cs[r][:, 0:n], op=ALU.mult),
             reads=[('kbf', r), ('cs', r)], writes=[('cs', r)])
        T.op('dve', lambda e: e.tensor_tensor(out=sn[r][:, 0:n], in0=psS[r][:, 0:n], in1=sn[r][:, 0:n], op=ALU.mult),
             reads=[('psS', r), ('sn', r)], writes=[('sn', r)])
        T.op('dve', lambda e: e.tensor_tensor(out=dst[:, dcol:dcol + n], in0=cs[r][:, 0:n], in1=sn[r][:, 0:n], op=ALU.add),
             reads=[('cs', r), ('sn', r)], writes=[dkey])

    def proj_seq(wt, wkey, items, moba):
        pend = None
        for (c0, n, dst, dkey, dcol) in items:
            pi = proj_mm(wt, wkey, c0, n)
            if pend is not None:
                rope_fin(*pend)
                pend = None
            r = proj_fin(pi, c0, n, moba, dst, dkey, dcol)
            if moba:
                pend = (r, n, dst, dkey, dcol)
        if pend is not None:
            rope_fin(*pend)

    def head_weights(hidx):
        moba = hidx < 8
        h = hidx % 8
        base = 0 if moba else 3072
        load_w(wk[:], 'wk', wv(w_in)[:, :, base + 1024 + h * 128: base + 1024 + (h + 1) * 128], KC, 128)
        load_w(wvv[:], 'wvv', wv(w_in)[:, :, base + 2048 + h * 128: base + 2048 + (h + 1) * 128], KC, 128)
        load_w(wq[:], 'wq', wv(w_in)[:, :, base + h * 128: base + (h + 1) * 128], KC, 128)

    def gate_a(hidx):
        T.op('dve', lambda e: e.tensor_reduce(out=ksum[:], in_=kT[:].rearrange("p (b k) -> p b k", k=256), axis=AX.X, op=ALU.add),
             reads=[('kT', g) for g in range(9)], writes=['ksum'])
        T.op('dve', lambda e: e.tensor_copy(out=ksumb[:], in_=ksum[:]), reads=['ksum'], writes=['ksumb'])

    def gate_b(hidx):
        for s in range(NQS):
            T.op('pe', lambda e: e.matmul(psO[:, s * NBLK:(s + 1) * NBLK], lhsT=qT[:, s * 128:(s + 1) * 128], rhs=ksumb[:], start=True, stop=True),
                 reads=[('qT', s // 3), 'ksumb'], writes=['psO'])
        T.op('dve', lambda e: e.tensor_tensor(out=gm[:], in0=psO[:, 0:NQS * NBLK], in1=pastb[:], op=ALU.add), reads=['psO', 'pastb'], writes=['gm'])
        for s in range(NQS):
            T.op('dve', lambda e: e.max(out=m8[:, s * 8:(s + 1) * 8], in_=gm[:, s * NBLK:(s + 1) * NBLK]), reads=['gm'], writes=['m8'])
        for s in range(NQS):
            T.op('dve', lambda e: e.tensor_scalar(out=ge[:, s * NBLK:(s + 1) * NBLK], in0=gm[:, s * NBLK:(s + 1) * NBLK],
                                                  scalar1=m8[:, s * 8 + 2:s * 8 + 3], scalar2=None, op0=ALU.is_ge), reads=['gm', 'm8'], writes=['ge'])
        T.op('dve', lambda e: e.tensor_tensor(out=ge[:], in0=ge[:], in1=past01[:], op=ALU.mult), reads=['ge', 'past01'], writes=['ge'])
        T.op('dve', lambda e: e.tensor_tensor(out=ge[:], in0=ge[:], in1=own01[:], op=ALU.max), reads=['ge', 'own01'], writes=['ge'])
        T.op('dve', lambda e: e.tensor_scalar(out=maskv[:, :, 0:NBLK], in0=ge[:].rearrange("p (a b) -> p a b", b=NBLK), scalar1=-1.0, scalar2=-NEGM, op0=ALU.add, op1=ALU.mult), reads=['ge'], writes=['maskv'])

    def gate_c(hidx):
        for qt in range(3):
            pb = tr_i[0] % 2
            tr_i[0] += 1
            for i in range(3):
                s = qt * 3 + i
                T.op('pe', lambda e: e.transpose(out=psT[pb][0:32, i * 128:(i + 1) * 128], in_=maskv[:, s, :], identity=identb[:]),
                     reads=['maskv', 'identb'], writes=[('psT', pb)])
            T.op('dve', lambda e: e.tensor_copy(out=maskT[0:32, qt * QW:(qt + 1) * QW], in_=psT[pb][0:32, 0:QW]), reads=[('psT', pb)], writes=[('maskT', qt)])

    def fox_bias(hidx):
        h = hidx % 8
        for qt in range(3):
            T.op('dve', lambda e: e.tensor_scalar(out=fb[:, qt, :], in0=ckk[:, :, h], scalar1=-1.0, scalar2=cref[:, h * 3 + qt:h * 3 + qt + 1],
                                                  op0=ALU.mult, op1=ALU.add), reads=['ckk', 'cref'], writes=[('fb', qt)])

    sring = [(psS[0], ('psS', 0)), (psS[1], ('psS', 1)), (psA[0], ('psA', 0)), (psA[1], ('psA', 1))]
    acc = [(psO, 'psO', psD, 'psD'), (psTf[0], ('psT', 0), psTf[1], ('psT', 1))]
    sr_i = [0]
    acc_i = [0]
    pt_i = [0]
    LOOK = 2

    def attention(hidx):
        moba = hidx < 8
        items = []
        for qt in range(3):
            first_q = QT0 + 3 * qt
            full = [kt for kt in range(0, first_q)]
            part = [first_q + i for i in range(3)]
            order = full[0:6] + part + full[6:]
            a4 = acc[acc_i[0] % 2]
            acc_i[0] += 1
            for idx, kt in enumerate(order):
                c0 = (kt - first_q) * 128 if kt >= first_q else 0
                items.append(dict(qt=qt, kt=kt, c0=c0, diag=(kt >= first_q), first=(idx == 0), last=(idx == len(order) - 1), acc=a4))
        n = len(items)

        def qk(it):
            kt, c0, qt = it['kt'], it['c0'], it['qt']
            sb, sk = sring[sr_i[0] % 4]
            sr_i[0] += 1
            it['sb'] = (sb, sk)
            T.op('pe', lambda e: e.matmul(sb[:, c0:QW], lhsT=kT[:, kt * 128:(kt + 1) * 128], rhs=qT[:, qt * QW + c0:(qt + 1) * QW], start=True, stop=(not moba)),
                 reads=[('kT', kt // 4), ('qT', qt)], writes=[sk])
            if moba:
                blk = kt // 2
                T.op('pe', lambda e: e.matmul(sb[:, c0:QW], lhsT=esel[:, blk * 128:(blk + 1) * 128], rhs=maskT[:, qt * QW + c0:(qt + 1) * QW], start=False, stop=True),
                     reads=['esel', ('maskT', qt)], writes=[sk])

        def ex(it):
            kt, c0, qt = it['kt'], it['c0'], it['qt']
            sb, sk = it['sb']
            pi = pt_i[0] % 3
            pt_i[0] += 1
            pt = Pt[pi]
            pk = ('Pt', pi)
            it['pt'] = (pt, pk)
            if moba:
                T.op('act', lambda e: e.activation(out=pt[:, c0:QW], in_=sb[:, c0:QW], func=AF.Exp, scale=SCALE), reads=[sk], writes=[pk])
            else:
                T.op('act', lambda e: e.activation(out=pt[:, c0:QW], in_=sb[:, c0:QW], func=AF.Exp, scale=SCALE, bias=fb[:, qt, kt:kt + 1]),
                     reads=[sk, ('fb', qt)], writes=[pk])
            if it['diag']:
                T.op('dve', lambda e: e.tensor_tensor(out=pt[:, c0:c0 + 128], in0=pt[:, c0:c0 + 128], in1=tri[:], op=ALU.mult), reads=[pk, 'tri'], writes=[pk])

        for i in range(min(LOOK, n)):
            qk(items[i])
        ex(items[0])
        for i in range(n):
            it = items[i]
            if i + LOOK < n:
                qk(items[i + LOOK])
            if i + 1 < n:
                ex(items[i + 1])
            kt, c0, qt = it['kt'], it['c0'], it['qt']
            aO, kO, aD, kD = it['acc']
            pt, pk = it['pt']
            T.op('pe', lambda e: e.matmul(aO[:, c0:QW], lhsT=V[:, kt, :], rhs=pt[:, c0:QW], start=it['first'], stop=it['last']),
                 reads=[('V', kt // 4), pk], writes=[kO])
            T.op('pe', lambda e: e.matmul(aD[:, c0:QW], lhsT=onesb[:], rhs=pt[:, c0:QW], start=it['first'], stop=it['last']),
                 reads=['onesb', pk], writes=[kD])
            if it['last']:
                mi = (hidx * 3 + qt) % 2
                T.op('dve', lambda e: e.tensor_scalar(out=rec[mi][:], in0=aD[:, 0:QW], scalar1=1e-30, scalar2=None, op0=ALU.add), reads=[kD], writes=[('rec', mi)])
                T.op('dve', lambda e: e.reciprocal(out=rec[mi][:], in_=rec[mi][:]), reads=[('rec', mi)], writes=[('rec', mi)])
                T.op('dve', lambda e: e.tensor_tensor(out=mixo[mi][:], in0=aO[:, 0:QW], in1=rec[mi][:], op=ALU.mult), reads=[kO, ('rec', mi)], writes=[('mixo', mi)])
                T.dma(mixT_d[hidx, :, qt * QW:(qt + 1) * QW], mixo[mi][:], reads=[('mixo', mi)], writes=[('mixT_d', hidx, qt)])

    def v_proj(gsel):
        pend = None

        def fin(vb, c0, n, gi):
            pb = tr_i[0] % 2
            tr_i[0] += 1
            for s in range(n // 128):
                T.op('pe', lambda e: e.transpose(out=psT[pb][:, s * 128:(s + 1) * 128], in_=vtmp[vb][:, s * 128:(s + 1) * 128], identity=identb[:]),
                     reads=[('vtmp', vb), 'identb'], writes=[('psT', pb)])
            T.op('dve', lambda e: e.tensor_copy(out=V[:, c0 // 128:(c0 + n) // 128, :], in_=psT[pb][:, 0:n].rearrange("p (a b) -> p a b", b=128)),
                 reads=[('psT', pb)], writes=[('V', gi)])

        for gi in gsel:
            c0, n = groups[gi]
            pi = proj_mm(wvv, 'wvv', c0, n)
            if pend is not None:
                fin(*pend)
            vb = gi % 2
            T.op('act', lambda e: e.activation(out=vtmp[vb][:, 0:n], in_=psA[pi][:, 0:n], func=AF.Copy), reads=[('psA', pi)], writes=[('vtmp', vb)])
            pend = (vb, c0, n, gi)
        fin(*pend)

    groups = [(g * 512, 512) for g in range(8)] + [(4096, 256)]
    head_weights(0)
    n_heads = 16
    for hidx in range(n_heads):
        moba = hidx < 8
        proj_seq(wk, 'wk', [(c0, n, kT, ('kT', gi), c0) for gi, (c0, n) in enumerate(groups)], moba)
        if moba:
            gate_a(hidx)
        else:
            fox_bias(hidx)
        proj_seq(wq, 'wq', [(QT0 * 128 + qt * QW, QW, qT, ('qT', qt), qt * QW) for qt in range(3)], moba)
        v_proj(range(0, 5))
        if moba:
            gate_b(hidx)
        v_proj(range(5, 9))
        if moba:
            gate_c(hidx)
        if DEBUG:
            T.dma(dbg['kT'][hidx], kT[:], reads=[('kT', g) for g in range(9)])
            T.dma(dbg['V'][hidx], V[:].rearrange("p a b -> p (a b)"), reads=[('V', g) for g in range(9)])
            T.dma(dbg['qT'][hidx], qT[:], reads=[('qT', g) for g in range(3)])
        if hidx + 1 < n_heads:
            head_weights(hidx + 1)
        attention(hidx)
    T.barrier()
    es.close()
    es_A.close()
    if DEBUG:
        T.dma(dbg['mixT'], mixT_d, reads=[])
        T.barrier()

    esF = ExitStack()
    actT = S(esF, "actT", [128, KC, NQ], BF16, side="right")
    xn3T = actT
    esB = ExitStack()
    hres = S(esB, "hres", [128, NQS, D], F32)
    junk = S(esB, "junk", [128, D], BF16)
    xnb = [S(esB, "xnb%d" % i, [128, D], BF16) for i in range(2)]
    grep = S(esB, "grep", [128, D], F32)

    K2T = S(esB, "K2T", [128, 4, 256], BF16)
    V2 = S(esB, "V2", [128, 2, 512], BF16)
    wgrp0 = S(esB, "wgrp0", [128, KC, 512], BF16)
    es = ExitStack()
    memT = S(es, "memT", [128, KC, 256], BF16)
    xt = [S(es, "xt0", [128, D], F32)]
    wckv = S(es, "wckv", [128, KC, 1024], BF16)
    load_g(g_mem)
    load_w(wckv[:], 'wckv', wv(w_ckv), KC, 1024, engs=('act',))
    load_w(wgrp0[:], ('wgrp', 0), wv(w_o)[:, :, 0:512], KC, 512, engs=('act',))

    def _ldm(mt):
        return lambda: T.dma(xt[0][:], memx[mt * 128:(mt + 1) * 128, :], writes=[('xt', 0)])
    norm_seq([(_ldm(mt), xt[0][:], ('xt', 0), 'memT', mt * 128) for mt in range(2)], memT)
    T.dma(actT[:], mixT_d.rearrange("h p n -> p h n"), writes=[('actT', s) for s in range(NQS)])
    for s in range(NQS):
        T.dma(hres[:, s, :], xc[(QT0 + s) * 128:(QT0 + s + 1) * 128, :], writes=[('hres', s)])
    for h in range(4):
        pi = pa_i[0] % 2
        pa_i[0] += 1
        for kc in range(KC):
            T.op('pe', lambda e: e.matmul(psA[pi][:, 0:256], lhsT=wckv[:, kc, h * 128:(h + 1) * 128], rhs=memT[:, kc, :], start=(kc == 0), stop=(kc == KC - 1)),
                 reads=['wckv', 'memT'], writes=[('psA', pi)])
        T.op('act', lambda e: e.activation(out=K2T[:, h, :], in_=psA[pi][:, 0:256], func=AF.Copy), reads=[('psA', pi)], writes=['K2T'])
    for mt in range(2):
        pi = pa_i[0] % 2
        pa_i[0] += 1
        for kc in range(KC):
            T.op('pe', lambda e: e.matmul(psA[pi][:, :], lhsT=memT[:, kc, mt * 128:(mt + 1) * 128], rhs=wckv[:, kc, 512:1024], start=(kc == 0), stop=(kc == KC - 1)),
                 reads=['wckv', 'memT'], writes=[('psA', pi)])
        T.op('act', lambda e: e.activation(out=V2[:, mt, :], in_=psA[pi][:, :], func=AF.Copy), reads=[('psA', pi)], writes=['V2'])
    T.barrier()
    es.close()

    es = ExitStack()
    wcq = S(es, "wcq", [128, KC, 512], BF16)
    wco = S(es, "wco", [128, 4, D], BF16)
    es2 = ExitStack()
    wgrp = [wgrp0, S(es2, "wgrp1", [128, KC, 512], BF16)]
    load_g(g_x)
    npipe = NormPipe(actT)
    for cg in range(4):
        wb = cg % 2
        if cg + 1 < 4:
            load_w(wgrp[1 - wb][:], ('wgrp', 1 - wb), wv(w_o)[:, :, (cg + 1) * 512:(cg + 2) * 512], KC, 512, engs=('pool', 'act'))
        if cg == 2:
            load_w(wcq[:], 'wcq', wv(w_cq), KC, 512, engs=('pool', 'act'))
            load_w(wco[:], 'wco', wv(w_co), 4, D, engs=('pool', 'act'))
        for s in range(NQS):
            pi = pa_i[0] % 2
            pa_i[0] += 1
            for kc in range(KC):
                T.op('pe', lambda e: e.matmul(psA[pi][:, :], lhsT=actT[:, kc, s * 128:(s + 1) * 128], rhs=wgrp[wb][:, kc, :], start=(kc == 0), stop=(kc == KC - 1)),
                     reads=[('actT', s), ('wgrp', wb)], writes=[('psA', pi)])
            T.op('dve', lambda e: e.tensor_tensor(out=hres[:, s, cg * 512:(cg + 1) * 512], in0=psA[pi][:, :], in1=hres[:, s, cg * 512:(cg + 1) * 512], op=ALU.add),
                 reads=[('psA', pi), ('hres', s)], writes=[('hres', s)])
            if cg == 3 and not DEBUG:
                npipe.push(hres[:, s, :], ('hres', s), ('actT', s), s * 128)
    if DEBUG:
        T.dma(dbg['h1'].rearrange("s p n -> p s n"), hres[:], reads=[('hres', s) for s in range(NQS)])
        for s in range(NQS):
            npipe.push(hres[:, s, :], ('hres', s), ('actT', s), s * 128)
    npipe.flush()
    T.barrier()
    es2.close()

    xn2T = actT
    es2 = ExitStack()
    q2T = S(es2, "q2T", [128, 4, NQ], BF16)
    o2T = S(es2, "o2T", [128, 4, NQ], BF16)
    Pt = [S(es2, "Pt%d" % i, [128, QW], BF16) for i in range(3)]
    rec = S(es2, "rec", [128, QW], F32)
    load_g(g_ffn)
    for h in range(4):
        for qt in range(3):
            pi = pa_i[0] % 2
            pa_i[0] += 1
            for kc in range(KC):
                T.op('pe', lambda e: e.matmul(psA[pi][:, 0:QW], lhsT=wcq[:, kc, h * 128:(h + 1) * 128], rhs=xn2T[:, kc, qt * QW:(qt + 1) * QW], start=(kc == 0), stop=(kc == KC - 1)),
                     reads=['wcq'] + [('actT', 3 * qt + i) for i in range(3)], writes=[('psA', pi)])
            T.op('act', lambda e: e.activation(out=q2T[:, h, qt * QW:(qt + 1) * QW], in_=psA[pi][:, 0:QW], func=AF.Copy), reads=[('psA', pi)], writes=[('q2T', h, qt)])
    pt_i = 0
    for h in range(4):
        for qt in range(3):
            for mt in range(2):
                T.op('pe', lambda e: e.matmul(psS[mt][:, 0:QW], lhsT=K2T[:, h, mt * 128:(mt + 1) * 128], rhs=q2T[:, h, qt * QW:(qt + 1) * QW], start=True, stop=True),
                     reads=['K2T', ('q2T', h, qt)], writes=[('psS', mt)])
            pts = []
            for mt in range(2):
                pt = Pt[pt_i % 3]
                pk = ('Pt', pt_i % 3)
                pt_i += 1
                pts.append((pt, pk))
                T.op('act', lambda e: e.activation(out=pt[:], in_=psS[mt][:, 0:QW], func=AF.Exp, scale=SCALE), reads=[('psS', mt)], writes=[pk])
            for mt in range(2):
                pt, pk = pts[mt]
                T.op('pe', lambda e: e.matmul(psO[:, 0:QW], lhsT=V2[:, mt, h * 128:(h + 1) * 128], rhs=pt[:], start=(mt == 0), stop=(mt == 1)), reads=['V2', pk], writes=['psO'])
                T.op('pe', lambda e: e.matmul(psD[:, 0:QW], lhsT=onesb[:], rhs=pt[:], start=(mt == 0), stop=(mt == 1)), reads=['onesb', pk], writes=['psD'])
            T.op('dve', lambda e: e.reciprocal(out=rec[:], in_=psD[:, 0:QW]), reads=['psD'], writes=['rec'])
            T.op('dve', lambda e: e.tensor_tensor(out=o2T[:, h, qt * QW:(qt + 1) * QW], in0=psO[:, 0:QW], in1=rec[:], op=ALU.mult), reads=['psO', 'rec'], writes=[('o2T', qt)])
    npipe = NormPipe(actT)

    def _after(s):
        def f():
            if s == 0:
                T.op('dve', lambda e: e.tensor_scalar(out=actT[:, :, 0:128], in0=actT[:, :, 0:128], scalar1=halov[:, 0:1], scalar2=None, op0=ALU.mult),
                     reads=[('actT', 0), 'halov'], writes=[('actT', 0)])
            T.dma(hres_d[s], hres[:, s, :], reads=[('hres', s)], writes=[('hd', s, cg) for cg in range(4)])
        return f

    for s in range(NQS):
        for cg in range(4):
            pi = pa_i[0] % 2
            pa_i[0] += 1
            for kc in range(4):
                T.op('pe', lambda e: e.matmul(psA[pi][:, :], lhsT=o2T[:, kc, s * 128:(s + 1) * 128], rhs=wco[:, kc, cg * 512:(cg + 1) * 512], start=(kc == 0), stop=(kc == 3)),
                     reads=[('o2T', s // 3), 'wco'], writes=[('psA', pi)])
            T.op('dve', lambda e: e.tensor_tensor(out=hres[:, s, cg * 512:(cg + 1) * 512], in0=psA[pi][:, :], in1=hres[:, s, cg * 512:(cg + 1) * 512], op=ALU.add),
                 reads=[('psA', pi), ('hres', s)], writes=[('hres', s)])
        if not DEBUG:
            npipe.push(hres[:, s, :], ('hres', s), ('actT', s), s * 128, after=_after(s))
    if DEBUG:
        T.dma(dbg['h2'].rearrange("s p n -> p s n"), hres[:], reads=[('hres', s) for s in range(NQS)])
        for s in range(NQS):
            npipe.push(hres[:, s, :], ('hres', s), ('actT', s), s * 128, after=_after(s))
    npipe.flush()
    T.barrier()
    es2.close()
    es.close()
    esB.close()

    AKEYS = [('actT', s_) for s_ in range(NQS)]
    es = ExitStack()
    aT = S(es, "aT", [128, NFC, 1024], BF16)
    wd0 = S(es, "wd0", [128, NFC, 512], BF16)
    es2 = ExitStack()
    wgu = [S(es2, "wgu%d" % i, [128, KC, 256], BF16) for i in range(2)]
    convw = S(es2, "convw", [128, 88, 3], F32)
    convb = S(es2, "convb", [128, 88], F32)
    yg = [S(es2, "yg%d" % i, [128, 344], F32) for i in range(2)]
    yu = [S(es2, "yu%d" % i, [128, 344], F32) for i in range(2)]
    sg = [S(es2, "sg%d" % i, [128, 344], F32) for i in range(2)]
    T.dma(convw[:].rearrange("p a b -> p (a b)"), convw_d, writes=['convw'])
    T.dma(convb[:], convb_d, writes=['convb'])
    it = 0
    def load_gu(c):
        wb = c % 2
        load_w(wgu[wb][:, :, 0:128], ('wgu', wb), wv(w_up)[:, :, c * 128:(c + 1) * 128], KC, 128, engs=('pool',))
        load_w(wgu[wb][:, :, 128:256], ('wgu', wb), wv(w_up)[:, :, DFF + c * 128:DFF + (c + 1) * 128], KC, 128, engs=('act',))

    load_gu(0)
    for c in range(NFC):
        wb = c % 2
        for tt in range(3):
            o0 = 342 * tt
            o1 = min(1024, 342 * (tt + 1))
            no = o1 - o0
            n = no + 2
            c0 = 128 + o0 - 2
            pg, pu = (psA[0], psA[1]) if it % 2 == 0 else (psS[0], psS[1])
            kg, ku = (('psA', 0), ('psA', 1)) if it % 2 == 0 else (('psS', 0), ('psS', 1))
            b = it % 2
            it += 1
            for kc in range(KC):
                T.op('pe', lambda e: e.matmul(pg[:, 0:n], lhsT=wgu[wb][:, kc, 0:128], rhs=xn3T[:, kc, c0:c0 + n], start=(kc == 0), stop=(kc == KC - 1)),
                     reads=[('wgu', wb)] + AKEYS, writes=[kg])
            for kc in range(KC):
                T.op('pe', lambda e: e.matmul(pu[:, 0:n], lhsT=wgu[wb][:, kc, 128:256], rhs=xn3T[:, kc, c0:c0 + n], start=(kc == 0), stop=(kc == KC - 1)),
                     reads=[('wgu', wb)] + AKEYS, writes=[ku])
            for (pp, pkey, yy, ykey, ch) in [(pg, kg, yg[b], ('yg', b), c), (pu, ku, yu[b], ('yu', b), NFC + c)]:
                T.op('act', lambda e: e.activation(out=yy[:, 0:no], in_=pp[:, 2:n], func=AF.Identity, scale=convw[:, ch, 2:3], bias=convb[:, ch:ch + 1]),
                     reads=[pkey, 'convw', 'convb'], writes=[ykey])
                T.op('dve', lambda e: e.scalar_tensor_tensor(out=yy[:, 0:no], in0=pp[:, 1:n - 1], scalar=convw[:, ch, 1:2], in1=yy[:, 0:no], op0=ALU.mult, op1=ALU.add),
                     reads=[pkey, 'convw', ykey], writes=[ykey])
                T.op('dve', lambda e: e.scalar_tensor_tensor(out=yy[:, 0:no], in0=pp[:, 0:n - 2], scalar=convw[:, ch, 0:1], in1=yy[:, 0:no], op0=ALU.mult, op1=ALU.add),
                     reads=[pkey, 'convw', ykey], writes=[ykey])
            T.op('act', lambda e: e.activation(out=sg[b][:, 0:no], in_=yg[b][:, 0:no], func=AF.Silu), reads=[('yg', b)], writes=[('sg', b)])
            T.op('dve', lambda e: e.tensor_tensor(out=aT[:, c, o0:o1], in0=sg[b][:, 0:no], in1=yu[b][:, 0:no], op=ALU.mult),
                 reads=[('sg', b), ('yu', b)], writes=[('aT', c)])
            if tt == 0 and c + 1 < NFC:
                load_gu(c + 1)
            if tt == 1 and NFC - 14 <= c < NFC - 3:
                k_ = c - (NFC - 14)
                load_w(wd0[:, 4 * k_:4 * k_ + 4, :], ('wd', 0), wv(w_down)[:, 4 * k_:4 * k_ + 4, 0:512], 4, 512, engs=('pool',))
    T.barrier()
    es2.close()
    esF.close()

    es2 = ExitStack()
    wd = [wd0, S(es2, "wd1", [128, NFC, 512], BF16)]
    h2t = [S(es2, "h2t%d" % i, [128, 512], F32) for i in range(8)]
    for cg in range(4):
        wb = cg % 2
        for s in range(8):
            T.dma(h2t[s][:], hres_d[1 + s, :, cg * 512:(cg + 1) * 512], reads=[('hd', 1 + s, cg)], writes=[('h2t', s)])
        if cg + 1 < 4:
            load_w(wd[1 - wb][:], ('wd', 1 - wb), wv(w_down)[:, :, (cg + 1) * 512:(cg + 2) * 512], NFC, 512, engs=('act', 'act', 'pool'))
        for s in range(8):
            pi = pa_i[0] % 2
            pa_i[0] += 1
            for fc in range(NFC):
                T.op('pe', lambda e: e.matmul(psA[pi][:, :], lhsT=aT[:, fc, s * 128:(s + 1) * 128], rhs=wd[wb][:, fc, :], start=(fc == 0), stop=(fc == NFC - 1)),
                     reads=[('aT', fc), ('wd', wb)], writes=[('psA', pi)])
            T.op('dve', lambda e: e.tensor_tensor(out=h2t[s][:], in0=psA[pi][:, :], in1=h2t[s][:], op=ALU.add), reads=[('psA', pi), ('h2t', s)], writes=[('h2t', s)])
            T.dma(hres_d[1 + s, :, cg * 512:(cg + 1) * 512], h2t[s][:], reads=[('h2t', s)], writes=[('hd', 1 + s, cg)])
    T.barrier()
    es2.close()
    es.close()
    if DEBUG:
        T.dma(dbg['h3'], hres_d, reads=[])
        T.barrier()

    es = ExitStack()
    xt = [S(es, "xt%d" % i, [128, D], F32) for i in range(3)]
    yo = [S(es, "yo%d" % i, [128, D], F32) for i in range(2)]
    junk = S(es, "junk", [128, D], BF16)
    grep = S(es, "grep", [128, D], F32)
    load_g(g_fin)

    def _ldf(s_):
        T.dma(xt[s_ % 3][:], hres_d[1 + s_], reads=[('hd', 1 + s_, cg) for cg in range(4)], writes=[('xt', s_ % 3)])
    _ldf(0)
    _ldf(1)
    for s in range(8):
        b = s % 2
        xb_ = s % 3
        rs, sk = rms_stats(xt[xb_][:], ('xt', xb_))
        T.op('dve', lambda e: e.scalar_tensor_tensor(out=yo[b][:], in0=xt[xb_][:], scalar=rs, in1=grep[:], op0=ALU.mult, op1=ALU.mult),
             reads=[('xt', xb_), sk, 'grep'], writes=[('yo', b)])
        if s + 2 < 8:
            _ldf(s + 2)
        T.dma(y[s * 128:(s + 1) * 128, :], yo[b][:], reads=[('yo', b)])
    T.barrier(engines=('sp',))
    T.barrier(engines=('pe', 'act', 'dve', 'pool'))
    es.close()
    es_top.close()
    return nc, T


_CACHE = {}


def _rope_tables(pos):
    half = 64
    inv = (np.float32(10000.0) ** (-(np.arange(half, dtype=np.float32)) / np.float32(half))).astype(np.float32)
    ang = (pos.astype(np.float32)[None, :] * inv[:, None]).astype(np.float32)
    c = np.cos(ang).astype(np.float32)
    s = np.sin(ang).astype(np.float32)
    cosT = np.concatenate([c, c], axis=0)
    sinT = np.concatenate([-s, s], axis=0)
    return np.ascontiguousarray(cosT), np.ascontiguousarray(sinT)


def _core_inputs(j, xb, memb, shared):
    others = [i for i in range(4) if i != j]
    xo = [xb[i * 1024:(i + 1) * 1024] for i in others]
    if j > 0:
        halo = xb[j * 1024 - 256:j * 1024]
        halo_pos = np.arange(j * 1024 - 256, j * 1024)
    else:
        halo = np.zeros((256, D), np.float32)
        halo_pos = np.arange(256)
    own = xb[j * 1024:(j + 1) * 1024]
    xcx = np.ascontiguousarray(np.concatenate(xo + [halo, own], axis=0))
    pos = np.concatenate([np.arange(i * 1024, (i + 1) * 1024) for i in others] + [halo_pos, np.arange(j * 1024, (j + 1) * 1024)])
    cosT, sinT = _rope_tables(pos)
    valid = np.zeros(NCT, np.float32)
    for kt in range(24):
        i = others[kt // 8]
        valid[kt] = 1.0 if (i < j and not (i == j - 1 and kt % 8 >= 6)) else 0.0
    valid[24:26] = 1.0 if j > 0 else 0.0
    valid[26:] = 1.0
    kbias = np.tile(((1.0 - valid) * NEGM)[None, :], (128, 1)).astype(np.float32)
    lfvalid = np.tile(valid[None, :], (8, 1)).astype(np.float32)
    halov = np.full((128, 1), 1.0 if j > 0 else 0.0, np.float32)
    past = np.zeros((NQS, NBLK), np.float32)
    ownm = np.zeros((NQS, NBLK), np.float32)
    for s in range(NQS):
        tb = (QT0 + s) // 2
        ownm[s, tb] = 1.0
        for b in range(NBLK):
            if b < 12:
                i = others[b // 4]
                ok = (i < j) and not (i == j - 1 and b % 4 == 3)
            elif b == 12:
                ok = (j > 0) and tb > 12
            else:
                ok = b < tb
            past[s, b] = 1.0 if ok else 0.0
    pastbias = ((1.0 - past) * -1e30).astype(np.float32)
    rep = lambda a: np.ascontiguousarray(np.tile(a.reshape(1, -1), (128, 1)).astype(np.float32))
    m = dict(shared)
    m.update({"xc": xcx, "memx": np.ascontiguousarray(memb), "cosT": cosT, "sinT": sinT, "kbias": kbias, "lfvalid": lfvalid,
              "halov": halov, "pastbias": rep(pastbias), "past01": rep(past), "own01": rep(ownm)})
    return m


def kernel(x, mem, attn_norm_g, w_in, b_f, w_o, xattn_norm_g, mem_norm_g, w_cq, w_ckv, w_co, ffn_norm_g,
           w_up, conv_w, conv_b, w_down, final_norm_g):
    f = lambda a: np.ascontiguousarray(np.asarray(a, dtype=np.float32))
    x = f(x); mem = f(mem)
    esel = np.zeros((128, NBLK, 128), np.float32)
    for b in range(NBLK):
        esel[b, b, :] = 1.0
    rp = np.zeros((128, 128), np.float32)
    for m_ in range(128):
        rp[(m_ + 64) % 128, m_] = 1.0
    tri = (np.arange(128)[:, None] <= np.arange(128)[None, :]).astype(np.float32)
    cw = f(conv_w)[0]
    shared = {
        "g_attn": f(attn_norm_g).reshape(1, D), "g_x": f(xattn_norm_g).reshape(1, D), "g_mem": f(mem_norm_g).reshape(1, D),
        "g_ffn": f(ffn_norm_g).reshape(1, D), "g_fin": f(final_norm_g).reshape(1, D),
        "w_in": f(w_in)[0], "w_o": f(w_o)[0], "w_cq": f(w_cq)[0], "w_ckv": f(w_ckv)[0], "w_co": f(w_co)[0],
        "w_up": f(w_up)[0], "w_down": f(w_down)[0],
        "convw": np.ascontiguousarray(cw.reshape(3, 88, 128).transpose(2, 1, 0).reshape(128, 88 * 3)),
        "convb": np.ascontiguousarray(f(conv_b)[0].reshape(88, 128).T),
        "bf": f(b_f).reshape(8, 1),
        "identb": np.eye(128, dtype=np.float32).astype(NPBF), "rperm": rp.astype(NPBF), "onesb": np.ones((128, 128), NPBF),
        "tri": tri.astype(NPBF), "esel": esel.reshape(128, NBLK * 128).astype(NPBF),
        "ident8": np.eye(8, dtype=np.float32), "ones8": np.ones((8, 128), np.float32),
    }
    if 'nc' not in _CACHE:
        _CACHE['nc'] = build_program()
    nc, T = _CACHE['nc']
    in_maps = []
    for c in range(8):
        b, j = c // 4, c % 4
        in_maps.append(_core_inputs(j, x[b], mem[b], shared))
    res = run_bass_kernel_spmd(nc, in_maps, core_ids=list(range(8)))
    _CACHE['last'] = res
    out = np.zeros((2, 4096, D), np.float32)
    for c in range(8):
        b, j = c // 4, c % 4
        out[b, j * 1024:(j + 1) * 1024] = res.results[c]["y"]
    return out
```

```python
import os
import numpy as np
import ml_dtypes
import concourse.bass as bass
import concourse.mybir as mybir
from concourse.bass_utils import run_bass_kernel_spmd
from contextlib import ExitStack

F32 = mybir.dt.float32
BF16 = mybir.dt.bfloat16
AF = mybir.ActivationFunctionType
ALU = mybir.AluOpType
AX = mybir.AxisListType
NPBF = ml_dtypes.bfloat16

D = 2048
KC = 16
NCT = 34
NCTX = NCT * 128
QT0 = 25
NQS = 9
NQ = NQS * 128
QW = 384
NBLK = 17
SCALE = 1.0 / float(np.sqrt(128.0))
NEGM = -30000.0
DFF = 5632
NFC = 44
DEBUG = bool(int(os.environ.get("MK_DEBUG", "0")))
STOP_AFTER = os.environ.get("MK_STOP", "")


class Tracker:
    def __init__(self, nc, n_dma_sems=28):
        self.nc = nc
        self.eng = {'pe': nc.tensor, 'act': nc.scalar, 'dve': nc.vector, 'pool': nc.gpsimd, 'sp': nc.sync}
        self.sem = {k: nc.alloc_semaphore('sem_' + k) for k in ['pe', 'act', 'dve', 'pool']}
        self.cnt = {k: 0 for k in self.sem}
        self.dma_sems = [nc.alloc_semaphore('dsem%d' % i) for i in range(n_dma_sems)]
        self.dma_cnt = [0] * n_dma_sems
        self.dma_rr = 0
        self.known = {}
        self.lastw = {}
        self.readers = {}
        self.semobj = dict(self.sem)
        for i, s in enumerate(self.dma_sems):
            self.semobj[('d', i)] = s
        self.n_ops = 0

    def _wait(self, e, semkey, val):
        if semkey == 'pe' and e == 'pe':
            return
        kk = (e, semkey)
        if self.known.get(kk, 0) >= val:
            return
        self.known[kk] = val
        self.eng[e].wait_ge(self.semobj[semkey], val)

    def _deps(self, e, reads, writes):
        for r in reads:
            w = self.lastw.get(r)
            if w is not None:
                self._wait(e, *w)
        for r in writes:
            w = self.lastw.get(r)
            if w is not None:
                self._wait(e, *w)
            rd = self.readers.get(r)
            if rd:
                for sk, v in rd.items():
                    self._wait(e, sk, v)

    def _record(self, tok, reads, writes):
        sk, v = tok
        for r in reads:
            d = self.readers.setdefault(r, {})
            if d.get(sk, 0) < v:
                d[sk] = v
        for r in writes:
            self.lastw[r] = tok
            self.readers[r] = {}

    def op(self, e, fn, reads=(), writes=()):
        self._deps(e, reads, writes)
        ins = fn(self.eng[e])
        self.cnt[e] += 1
        ins.then_inc(self.sem[e], 1)
        tok = (e, self.cnt[e])
        self._record(tok, reads, writes)
        self.n_ops += 1
        return tok

    def dma(self, out, in_, reads=(), writes=(), q='sp'):
        self._deps(q, reads, writes)
        i = self.dma_rr
        self.dma_rr = (self.dma_rr + 1) % len(self.dma_sems)
        if self.dma_cnt[i] > 0:
            self._wait(q, ('d', i), self.dma_cnt[i])
        self.dma_cnt[i] += 16
        self.eng[q].dma_start(out=out, in_=in_).then_inc(self.dma_sems[i], 16)
        tok = (('d', i), self.dma_cnt[i])
        self._record(tok, reads, writes)
        self.n_ops += 1
        return tok

    def barrier(self, engines=('pe', 'act', 'dve', 'pool', 'sp')):
        toks = [(k, c) for k, c in self.cnt.items() if c > 0]
        toks += [(('d', i), c) for i, c in enumerate(self.dma_cnt) if c > 0]
        for e in engines:
            for t in toks:
                if t[0] == e:
                    continue
                self._wait(e, *t)
        self.lastw = {}
        self.readers = {}


def build_program():
    nc = bass.Bass("TRN2", target_bir_lowering=False, dynamic_dma_scratch_size=512)
    T = Tracker(nc)

    def din(name, shape, dt=F32):
        return nc.dram_tensor(name, list(shape), dt, kind="ExternalInput").ap()

    xc = din("xc", [NCTX, D])
    memx = din("memx", [256, D])
    g_attn = din("g_attn", [1, D]); g_x = din("g_x", [1, D]); g_mem = din("g_mem", [1, D])
    g_ffn = din("g_ffn", [1, D]); g_fin = din("g_fin", [1, D])
    w_in = din("w_in", [D, 6152]); w_o = din("w_o", [D, D]); w_cq = din("w_cq", [D, 512])
    w_ckv = din("w_ckv", [D, 1024]); w_co = din("w_co", [512, D]); w_up = din("w_up", [D, 2 * DFF])
    w_down = din("w_down", [DFF, D])
    convw_d = din("convw", [128, 88 * 3]); convb_d = din("convb", [128, 88]); bf_d = din("bf", [8, 1])
    cos_d = din("cosT", [128, NCTX]); sin_d = din("sinT", [128, NCTX])
    kbias_d = din("kbias", [128, NCT]); lfv_d = din("lfvalid", [8, NCT]); halov_d = din("halov", [128, 1])
    pastb_d = din("pastbias", [128, NQS * NBLK]); past01_d = din("past01", [128, NQS * NBLK]); own01_d = din("own01", [128, NQS * NBLK])
    identb_d = din("identb", [128, 128], BF16); rperm_d = din("rperm", [128, 128], BF16)
    onesb_d = din("onesb", [128, 128], BF16); tri_d = din("tri", [128, 128], BF16)
    esel_d = din("esel", [128, NBLK * 128], BF16); ident8_d = din("ident8", [8, 8]); ones8_d = din("ones8", [8, 128])
    y = nc.dram_tensor("y", [1024, D], F32, kind="ExternalOutput").ap()
    mixT_d = nc.dram_tensor("mixT_d", [16, 128, NQ], BF16, kind="Internal").ap()
    hres_d = nc.dram_tensor("hres_d", [NQS, 128, D], F32, kind="Internal").ap()
    dbg = {}
    if DEBUG:
        dbg['xnT'] = nc.dram_tensor("dbg_xnT", [128, KC, NCTX], BF16, kind="ExternalOutput").ap()
        dbg['ck'] = nc.dram_tensor("dbg_ck", [128, NCT * 8], F32, kind="ExternalOutput").ap()
        dbg['kT'] = nc.dram_tensor("dbg_kT", [16, 128, NCTX], BF16, kind="ExternalOutput").ap()
        dbg['V'] = nc.dram_tensor("dbg_V", [16, 128, NCT * 128], BF16, kind="ExternalOutput").ap()
        dbg['qT'] = nc.dram_tensor("dbg_qT", [16, 128, NQ], BF16, kind="ExternalOutput").ap()
        dbg['mixT'] = nc.dram_tensor("dbg_mixT", [16, 128, NQ], BF16, kind="ExternalOutput").ap()
        dbg['h1'] = nc.dram_tensor("dbg_h1", [NQS, 128, D], F32, kind="ExternalOutput").ap()
        dbg['h2'] = nc.dram_tensor("dbg_h2", [NQS, 128, D], F32, kind="ExternalOutput").ap()
        dbg['h3'] = nc.dram_tensor("dbg_h3", [NQS, 128, D], F32, kind="ExternalOutput").ap()

    wv = lambda ap: ap.rearrange("(c p) n -> p c n", p=128)

    psA = [nc.alloc_psum_tensor("psA%d" % i, [128, 512], F32) for i in range(2)]
    psS = [nc.alloc_psum_tensor("psS%d" % i, [128, 512], F32) for i in range(2)]
    psTf = [nc.alloc_psum_tensor("psT%d" % i, [128, 512], F32) for i in range(2)]

    class _BV:
        def __init__(self, t):
            self.t = t

        def __getitem__(self, key):
            return self.t[:].bitcast(BF16)[key]
    psT = [_BV(t) for t in psTf]
    psO = nc.alloc_psum_tensor("psO", [128, 512], F32)
    psD = nc.alloc_psum_tensor("psD", [128, 512], F32)

    A = lambda name, shape, dt: nc.alloc_sbuf_tensor("sb_" + name, shape, dt)
    identb = A("identb", [128, 128], BF16); rperm = A("rperm", [128, 128], BF16)
    onesb = A("onesb", [128, 128], BF16); tri = A("tri", [128, 128], BF16)
    ident8 = A("ident8", [8, 8], F32); ones8 = A("ones8", [8, 128], F32)
    halov = A("halov", [128, 1], F32)
    for t_, d_, nm in [(identb, identb_d, 'identb'), (rperm, rperm_d, 'rperm'), (onesb, onesb_d, 'onesb'), (tri, tri_d, 'tri'),
                       (ident8, ident8_d, 'ident8'), (ones8, ones8_d, 'ones8'), (halov, halov_d, 'halov')]:
        T.dma(t_[:], d_, writes=[nm])
    wst = [A("wst%d" % i, [128, 2048], F32) for i in range(2)]
    wst_i = [0]
    stat = A("stat", [128, 2 * (NCT + 2 * NQS + 12)], F32)
    stat_i = [0]

    def load_w(dst, dkey, src, G, C, engs=('pool',)):
        gs = max(1, 2048 // C)
        g0 = 0
        k = 0
        while g0 < G:
            g1 = min(G, g0 + gs)
            i = wst_i[0] % 2
            wst_i[0] += 1
            st = wst[i][:, 0:(g1 - g0) * C].rearrange("p (g c) -> p g c", c=C)
            T.dma(st, src[:, g0:g1, :], writes=[('wst', i)])
            en = engs[k % len(engs)]
            k += 1
            if en == 'act':
                T.op('act', lambda e: e.activation(out=dst[:, g0:g1, :], in_=st, func=AF.Copy), reads=[('wst', i)], writes=[dkey])
            else:
                T.op(en, lambda e: e.tensor_copy(out=dst[:, g0:g1, :], in_=st), reads=[('wst', i)], writes=[dkey])
            g0 = g1

    def rms_sq(src, skey):
        k = stat_i[0]
        stat_i[0] += 1
        ss = stat[:, 2 * k:2 * k + 1]
        rs = stat[:, 2 * k + 1:2 * k + 2]
        sk = ('stat', k)
        T.op('act', lambda e: e.activation(out=junk[:], in_=src, func=AF.Square, accum_out=ss), reads=[skey], writes=['junk', sk])
        return ss, rs, sk

    def rms_fin(ss, rs, sk):
        T.op('dve', lambda e: e.tensor_scalar(out=rs, in0=ss, scalar1=1.0 / D, scalar2=1e-6, op0=ALU.mult, op1=ALU.add), reads=[sk], writes=[sk])
        T.op('act', lambda e: e.activation(out=rs, in_=rs, func=AF.Sqrt), reads=[sk], writes=[sk])
        T.op('dve', lambda e: e.reciprocal(out=rs, in_=rs), reads=[sk], writes=[sk])

    def rms_stats(src, skey):
        ss, rs, sk = rms_sq(src, skey)
        rms_fin(ss, rs, sk)
        return rs, sk

    tr_i = [0]

    def norm_mul(src, skey, rs, sk):
        b = tr_i[0] % 2
        xb = xnb[b]
        T.op('dve', lambda e: e.scalar_tensor_tensor(out=xb[:], in0=src, scalar=rs, in1=grep[:], op0=ALU.mult, op1=ALU.mult),
             reads=[skey, sk, 'grep'], writes=[('xnb', b)])
        return b

    def norm_tr(b, dstT, dkey, col):
        xb = xnb[b]
        for half in range(2):
            pb = tr_i[0] % 2
            tr_i[0] += 1
            pt = psT[pb]
            for k in range(8):
                kc = half * 8 + k
                T.op('pe', lambda e: e.transpose(out=pt[:, k * 128:(k + 1) * 128], in_=xb[:, kc * 128:(kc + 1) * 128], identity=identb[:]),
                     reads=[('xnb', b), 'identb'], writes=[('psT', pb)])
            src_v = pt[:, :].rearrange("p (a b) -> p a b", b=128)
            dst_v = dstT[:, half * 8:(half + 1) * 8, col:col + 128]
            T.op('act', lambda e: e.activation(out=dst_v, in_=src_v, func=AF.Copy), reads=[('psT', pb)], writes=[dkey])

    def norm_seq(items, dstT):
        pend = None
        for (load_fn, src, skey, dkey, col) in items:
            if pend is not None:
                b = norm_mul(pend[0], pend[1], pend[2], pend[3])
            if load_fn is not None:
                load_fn()
            ss, rs, sk = rms_sq(src, skey)
            if pend is not None:
                norm_tr(b, dstT, pend[4], pend[5])
            rms_fin(ss, rs, sk)
            pend = (src, skey, rs, sk, dkey, col)
        b = norm_mul(pend[0], pend[1], pend[2], pend[3])
        norm_tr(b, dstT, pend[4], pend[5])

    class NormPipe:
        def __init__(self, dstT):
            self.dstT = dstT
            self.statted = None
            self.mulled = None

        def _step(self, new):
            m = self.mulled
            if m is not None:
                norm_tr(m[0], self.dstT, m[1], m[2])
                if m[3] is not None:
                    m[3]()
                self.mulled = None
            p = self.statted
            if p is not None:
                b = norm_mul(p[0], p[1], p[2], p[3])
                self.mulled = (b, p[4], p[5], p[6])
                self.statted = None
            if new is not None:
                src, skey, dkey, col, after = new
                ss, rs, sk = rms_sq(src, skey)
                rms_fin(ss, rs, sk)
                self.statted = (src, skey, rs, sk, dkey, col, after)

        def push(self, src, skey, dkey, col, after=None):
            self._step((src, skey, dkey, col, after))

        def flush(self):
            self._step(None)
            self._step(None)

    def load_g(g_d):
        T.dma(grep[:], g_d.partition_broadcast(128), writes=['grep'])

    es_top = ExitStack()
    _uid = [0]

    def S(st, name, shape, dt, side=None):
        _uid[0] += 1
        return st.enter_context(nc.sbuf_tensor("sb%d_%s" % (_uid[0], name), list(shape), dt, side=side))

    es_A = ExitStack()
    xnT = S(es_A, "xnT", [128, KC, NCTX], BF16)
    ck = S(es_A, "ck", [128, NCT, 8], F32)
    ckk = S(es_A, "ckk", [128, NCT, 8], F32)
    cref = S(es_A, "cref", [128, 24], F32)
    kbias = S(es_A, "kbias", [128, NCT], F32)
    wzf = S(es_A, "wzf", [128, KC, 8], F32)
    wzb = S(es_A, "wzb", [128, KC, 8], BF16)
    bfn = S(es_A, "bfn", [8, 1], F32)
    lfv = S(es_A, "lfv", [8, NCT], F32)
    one8 = S(es_A, "one8", [8, 1], F32)
    rhsd = S(es_A, "rhsd", [8, 8, 3], F32)
    T.dma(kbias[:], kbias_d, writes=['kbias'])

    es = ExitStack()
    xt = [S(es, "xt%d" % i, [128, D], F32) for i in range(3)]
    junk = S(es, "junk", [128, D], BF16)
    xnb = [S(es, "xnb%d" % i, [128, D], BF16) for i in range(2)]
    grep = S(es, "grep", [128, D], F32)
    load_g(g_attn)
    T.dma(wzf[:], wv(w_in)[:, :, 6144:6152], writes=['wzf'])
    T.dma(bfn[:], bf_d, writes=['bfn'])
    T.dma(lfv[:], lfv_d, writes=['lfv'])
    npipe0 = NormPipe(xnT)
    for tl in range(NCT):
        b = tl % 3
        T.dma(xt[b][:], xc[tl * 128:(tl + 1) * 128, :], writes=[('xt', b)])
        npipe0.push(xt[b][:], ('xt', b), ('xnT', tl), tl * 128)
    npipe0.flush()
    if DEBUG:
        T.dma(dbg['xnT'], xnT[:], reads=[('xnT', tl) for tl in range(NCT)])

    T.barrier()
    es.close()
    es = ExitStack()
    zl = S(es, "zl", [8, NCTX], F32)
    cT = S(es, "cT", [8, NCTX], F32)
    T.op('pool', lambda e: e.tensor_copy(out=wzb[:], in_=wzf[:]), reads=['wzf'], writes=['wzb'])
    T.op('dve', lambda e: e.tensor_scalar(out=bfn[:], in0=bfn[:], scalar1=-1.0, scalar2=None, op0=ALU.mult), reads=['bfn'], writes=['bfn'])
    T.op('dve', lambda e: e.memset(one8[:], 1.0), writes=['one8'])
    groups = [(g * 512, 512) for g in range(8)] + [(4096, 256)]
    for gi, (c0, n) in enumerate(groups):
        pa = psA[gi % 2]
        for kc in range(KC):
            T.op('pe', lambda e: e.matmul(pa[0:8, 0:n], lhsT=wzb[:, kc, :], rhs=xnT[:, kc, c0:c0 + n], start=(kc == 0), stop=(kc == KC - 1)),
                 reads=['wzb'] + [('xnT', tl) for tl in range(c0 // 128, (c0 + n) // 128)], writes=[('psA', gi % 2)])
        T.op('act', lambda e: e.activation(out=zl[:, c0:c0 + n], in_=pa[0:8, 0:n], func=AF.Exp, scale=-1.0, bias=bfn[:]),
             reads=[('psA', gi % 2), 'bfn'], writes=['zl'])
    T.op('act', lambda e: e.activation(out=zl[:], in_=zl[:], func=AF.Ln, bias=1.0), reads=['zl'], writes=['zl'])
    T.op('dve', lambda e: e.scalar_tensor_tensor(out=zl[:].rearrange("p (a b) -> p a b", b=128), in0=zl[:].rearrange("p (a b) -> p a b", b=128),
                                                 scalar=-1.0, in1=lfv[:].unsqueeze(2).to_broadcast([8, NCT, 128]), op0=ALU.mult, op1=ALU.mult),
         reads=['zl', 'lfv'], writes=['zl'])
    T.op('dve', lambda e: e.tensor_tensor_scan(out=cT[:], data0=one8[:].to_broadcast([8, NCTX]), data1=zl[:], initial=0.0, op0=ALU.mult, op1=ALU.add),
         reads=['zl', 'one8'], writes=['cT'])
    for tl in range(NCT):
        T.op('pe', lambda e: e.transpose(out=psS[0][:, tl * 8:(tl + 1) * 8], in_=cT[:, tl * 128:(tl + 1) * 128], identity=ident8[:]),
             reads=['cT', 'ident8'], writes=[('psS', 0)])
    T.op('dve', lambda e: e.tensor_copy(out=ck[:], in_=psS[0][:, 0:NCT * 8].rearrange("p (a b) -> p a b", b=8)), reads=[('psS', 0)], writes=['ck'])
    T.op('dve', lambda e: e.tensor_tensor(out=ckk[:], in0=ck[:], in1=kbias[:].unsqueeze(2).to_broadcast([128, NCT, 8]), op=ALU.subtract),
         reads=['ck', 'kbias'], writes=['ckk'])
    for qt in range(3):
        tm = QT0 * 128 + qt * QW + QW // 2
        T.op('dve', lambda e: e.tensor_scalar(out=rhsd[:, :, qt], in0=ident8[:], scalar1=cT[:, tm:tm + 1], scalar2=None, op0=ALU.mult),
             reads=['cT', 'ident8'], writes=['rhsd'])
    T.op('pe', lambda e: e.matmul(psS[1][:, 0:24], lhsT=ones8[:], rhs=rhsd[:].rearrange("p a b -> p (a b)"), start=True, stop=True),
         reads=['ones8', 'rhsd'], writes=[('psS', 1)])
    T.op('dve', lambda e: e.tensor_copy(out=cref[:], in_=psS[1][:, 0:24]), reads=[('psS', 1)], writes=['cref'])
    if DEBUG:
        T.dma(dbg['ck'], ck[:].rearrange("p a b -> p (a b)"), reads=['ck'])
    T.barrier()
    es.close()

    es = ExitStack()
    wq = S(es, "wq", [128, KC, 128], BF16); wk = S(es, "wk", [128, KC, 128], BF16); wvv = S(es, "wvv", [128, KC, 128], BF16)
    kT = S(es, "kT", [128, NCTX], BF16)
    V = S(es, "V", [128, NCT, 128], BF16)
    qT = S(es, "qT", [128, NQ], BF16)
    vtmp = [S(es, "vtmp%d" % i, [128, 512], BF16) for i in range(2)]
    kbf = [S(es, "kbf%d" % i, [128, 512], BF16) for i in range(2)]
    cs = [S(es, "cs%d" % i, [128, 512], F32) for i in range(2)]
    sn = [S(es, "sn%d" % i, [128, 512], F32) for i in range(2)]
    Pt = [S(es, "Pt%d" % i, [128, QW], BF16) for i in range(3)]
    maskT = S(es, "maskT", [128, NQ], BF16)
    T.op('pool', lambda e: e.memset(maskT[:], 0.0), writes=[('maskT', 0), ('maskT', 1), ('maskT', 2)])
    fb = S(es, "fb", [128, 3, NCT], F32)
    rec = [S(es, "rec%d" % i, [128, QW], F32) for i in range(2)]
    mixo = [S(es, "mixo%d" % i, [128, QW], BF16) for i in range(2)]
    esel = S(es, "esel", [128, NBLK * 128], BF16)
    pastb = S(es, "pastb", [128, NQS * NBLK], F32); past01 = S(es, "past01", [128, NQS * NBLK], F32); own01 = S(es, "own01", [128, NQS * NBLK], F32)
    ksum = S(es, "ksum", [128, NBLK], F32); ksumb = S(es, "ksumb", [128, NBLK], BF16)
    gm = S(es, "gm", [128, NQS * NBLK], F32); m8 = S(es, "m8", [128, NQS * 8], F32); ge = S(es, "ge", [128, NQS * NBLK], F32)
    maskv = S(es, "maskv", [128, NQS, 32], BF16)
    T.op('dve', lambda e: e.memset(maskv[:], 0.0), writes=['maskv'])
    for t_, d_, nm in [(esel, esel_d, 'esel'), (pastb, pastb_d, 'pastb'), (past01, past01_d, 'past01'), (own01, own01_d, 'own01')]:
        T.dma(t_[:], d_, writes=[nm])

    rope_i = [0]
    pa_i = [0]

    def proj_mm(wt, wkey, c0, n):
        pi = pa_i[0] % 2
        pa_i[0] += 1
        pa = psA[pi]
        xkeys = [('xnT', tl) for tl in range(c0 // 128, (c0 + n) // 128)]
        for kc in range(KC):
            T.op('pe', lambda e: e.matmul(pa[:, 0:n], lhsT=wt[:, kc, :], rhs=xnT[:, kc, c0:c0 + n], start=(kc == 0), stop=(kc == KC - 1)),
                 reads=[wkey] + xkeys, writes=[('psA', pi)])
        return pi

    def proj_fin(pi, c0, n, moba, dst, dkey, dcol):
        pa = psA[pi]
        if not moba:
            T.op('act', lambda e: e.activation(out=dst[:, dcol:dcol + n], in_=pa[:, 0:n], func=AF.Copy), reads=[('psA', pi)], writes=[dkey])
            return
        r = rope_i[0] % 2
        rope_i[0] += 1
        T.dma(cs[r][:, 0:n], cos_d[:, c0:c0 + n], writes=[('cs', r)])
        T.dma(sn[r][:, 0:n], sin_d[:, c0:c0 + n], writes=[('sn', r)])
        T.op('act', lambda e: e.activation(out=kbf[r][:, 0:n], in_=pa[:, 0:n], func=AF.Copy), reads=[('psA', pi)], writes=[('kbf', r)])
        return r

    def rope_fin(r, n, dst, dkey, dcol):
        T.op('pe', lambda e: e.matmul(psS[r][:, 0:n], lhsT=rperm[:], rhs=kbf[r][:, 0:n], start=True, stop=True),
             reads=['rperm', ('kbf', r)], writes=[('psS', r)])
        T.op('pool', lambda e: e.tensor_tensor(out=cs[r][:, 0:n], in0=kbf[r][:, 0:n], in1=cs[r][:, 0:n], op=ALU.mult),
             reads=[('kbf', r), ('cs', r)], writes=[('cs', r)])
        T.op('dve', lambda e: e.tensor_tensor(out=sn[r][:, 0:n], in0=psS[r][:, 0:n], in1=sn[r][:, 0:n], op=ALU.mult),
             reads=[('psS', r), ('sn', r)], writes=[('sn', r)])
        T.op('dve', lambda e: e.tensor_tensor(out=dst[:, dcol:dcol + n], in0=cs[r][:, 0:n], in1=sn[r][:, 0:n], op=ALU.add),
             reads=[('cs', r), ('sn', r)], writes=[dkey])

    def proj_seq(wt, wkey, items, moba):
        pend = None
        for (c0, n, dst, dkey, dcol) in items:
            pi = proj_mm(wt, wkey, c0, n)
            if pend is not None:
                rope_fin(*pend)
                pend = None
            r = proj_fin(pi, c0, n, moba, dst, dkey, dcol)
            if moba:
                pend = (r, n, dst, dkey, dcol)
        if pend is not None:
            rope_fin(*pend)

    def head_weights(hidx):
        moba = hidx < 8
        h = hidx % 8
        base = 0 if moba else 3072
        load_w(wk[:], 'wk', wv(w_in)[:, :, base + 1024 + h * 128: base + 1024 + (h + 1) * 128], KC, 128)
        load_w(wvv[:], 'wvv', wv(w_in)[:, :, base + 2048 + h * 128: base + 2048 + (h + 1) * 128], KC, 128)
        load_w(wq[:], 'wq', wv(w_in)[:, :, base + h * 128: base + (h + 1) * 128], KC, 128)

    def gate_a(hidx):
        T.op('dve', lambda e: e.tensor_reduce(out=ksum[:], in_=kT[:].rearrange("p (b k) -> p b k", k=256), axis=AX.X, op=ALU.add),
             reads=[('kT', g) for g in range(9)], writes=['ksum'])
        T.op('dve', lambda e: e.tensor_copy(out=ksumb[:], in_=ksum[:]), reads=['ksum'], writes=['ksumb'])

    def gate_b(hidx):
        for s in range(NQS):
            T.op('pe', lambda e: e.matmul(psO[:, s * NBLK:(s + 1) * NBLK], lhsT=qT[:, s * 128:(s + 1) * 128], rhs=ksumb[:], start=True, stop=True),
                 reads=[('qT', s // 3), 'ksumb'], writes=['psO'])
        T.op('dve', lambda e: e.tensor_tensor(out=gm[:], in0=psO[:, 0:NQS * NBLK], in1=pastb[:], op=ALU.add), reads=['psO', 'pastb'], writes=['gm'])
        for s in range(NQS):
            T.op('dve', lambda e: e.max(out=m8[:, s * 8:(s + 1) * 8], in_=gm[:, s * NBLK:(s + 1) * NBLK]), reads=['gm'], writes=['m8'])
        for s in range(NQS):
            T.op('dve', lambda e: e.tensor_scalar(out=ge[:, s * NBLK:(s + 1) * NBLK], in0=gm[:, s * NBLK:(s + 1) * NBLK],
                                                  scalar1=m8[:, s * 8 + 2:s * 8 + 3], scalar2=None, op0=ALU.is_ge), reads=['gm', 'm8'], writes=['ge'])
        T.op('dve', lambda e: e.tensor_tensor(out=ge[:], in0=ge[:], in1=past01[:], op=ALU.mult), reads=['ge', 'past01'], writes=['ge'])
        T.op('dve', lambda e: e.tensor_tensor(out=ge[:], in0=ge[:], in1=own01[:], op=ALU.max), reads=['ge', 'own01'], writes=['ge'])
        T.op('dve', lambda e: e.tensor_scalar(out=maskv[:, :, 0:NBLK], in0=ge[:].rearrange("p (a b) -> p a b", b=NBLK), scalar1=-1.0, scalar2=-NEGM, op0=ALU.add, op1=ALU.mult), reads=['ge'], writes=['maskv'])

    def gate_c(hidx):
        for qt in range(3):
            pb = tr_i[0] % 2
            tr_i[0] += 1
            for i in range(3):
                s = qt * 3 + i
                T.op('pe', lambda e: e.transpose(out=psT[pb][0:32, i * 128:(i + 1) * 128], in_=maskv[:, s, :], identity=identb[:]),
                     reads=['maskv', 'identb'], writes=[('psT', pb)])
            T.op('dve', lambda e: e.tensor_copy(out=maskT[0:32, qt * QW:(qt + 1) * QW], in_=psT[pb][0:32, 0:QW]), reads=[('psT', pb)], writes=[('maskT', qt)])

    def fox_bias(hidx):
        h = hidx % 8
        for qt in range(3):
            T.op('dve', lambda e: e.tensor_scalar(out=fb[:, qt, :], in0=ckk[:, :, h], scalar1=-1.0, scalar2=cref[:, h * 3 + qt:h * 3 + qt + 1],
                                                  op0=ALU.mult, op1=ALU.add), reads=['ckk', 'cref'], writes=[('fb', qt)])

    sring = [(psS[0], ('psS', 0)), (psS[1], ('psS', 1)), (psA[0], ('psA', 0)), (psA[1], ('psA', 1))]
    acc = [(psO, 'psO', psD, 'psD'), (psTf[0], ('psT', 0), psTf[1], ('psT', 1))]
    sr_i = [0]
    acc_i = [0]
    pt_i = [0]
    LOOK = 2

    def attention(hidx):
        moba = hidx < 8
        items = []
        for qt in range(3):
            first_q = QT0 + 3 * qt
            full = [kt for kt in range(0, first_q)]
            part = [first_q + i for i in range(3)]
            order = full[0:6] + part + full[6:]
            a4 = acc[acc_i[0] % 2]
            acc_i[0] += 1
            for idx, kt in enumerate(order):
                c0 = (kt - first_q) * 128 if kt >= first_q else 0
                items.append(dict(qt=qt, kt=kt, c0=c0, diag=(kt >= first_q), first=(idx == 0), last=(idx == len(order) - 1), acc=a4))
        n = len(items)

        def qk(it):
            kt, c0, qt = it['kt'], it['c0'], it['qt']
            sb, sk = sring[sr_i[0] % 4]
            sr_i[0] += 1
            it['sb'] = (sb, sk)
            T.op('pe', lambda e: e.matmul(sb[:, c0:QW], lhsT=kT[:, kt * 128:(kt + 1) * 128], rhs=qT[:, qt * QW + c0:(qt + 1) * QW], start=True, stop=(not moba)),
                 reads=[('kT', kt // 4), ('qT', qt)], writes=[sk])
            if moba:
                blk = kt // 2
                T.op('pe', lambda e: e.matmul(sb[:, c0:QW], lhsT=esel[:, blk * 128:(blk + 1) * 128], rhs=maskT[:, qt * QW + c0:(qt + 1) * QW], start=False, stop=True),
                     reads=['esel', ('maskT', qt)], writes=[sk])

        def ex(it):
            kt, c0, qt = it['kt'], it['c0'], it['qt']
            sb, sk = it['sb']
            pi = pt_i[0] % 3
            pt_i[0] += 1
            pt = Pt[pi]
            pk = ('Pt', pi)
            it['pt'] = (pt, pk)
            if moba:
                T.op('act', lambda e: e.activation(out=pt[:, c0:QW], in_=sb[:, c0:QW], func=AF.Exp, scale=SCALE), reads=[sk], writes=[pk])
            else:
                T.op('act', lambda e: e.activation(out=pt[:, c0:QW], in_=sb[:, c0:QW], func=AF.Exp, scale=SCALE, bias=fb[:, qt, kt:kt + 1]),
                     reads=[sk, ('fb', qt)], writes=[pk])
            if it['diag']:
                T.op('dve', lambda e: e.tensor_tensor(out=pt[:, c0:c0 + 128], in0=pt[:, c0:c0 + 128], in1=tri[:], op=ALU.mult), reads=[pk, 'tri'], writes=[pk])

        for i in range(min(LOOK, n)):
            qk(items[i])
        ex(items[0])
        for i in range(n):
            it = items[i]
            if i + LOOK < n:
                qk(items[i + LOOK])
            if i + 1 < n:
                ex(items[i + 1])
            kt, c0, qt = it['kt'], it['c0'], it['qt']
            aO, kO, aD, kD = it['acc']
            pt, pk = it['pt']
            T.op('pe', lambda e: e.matmul(aO[:, c0:QW], lhsT=V[:, kt, :], rhs=pt[:, c0:QW], start=it['first'], stop=it['last']),
                 reads=[('V', kt // 4), pk], writes=[kO])
            T.op('pe', lambda e: e.matmul(aD[:, c0:QW], lhsT=onesb[:], rhs=pt[:, c0:QW], start=it['first'], stop=it['last']),
                 reads=['onesb', pk], writes=[kD])
            if it['last']:
                mi = (hidx * 3 + qt) % 2
                T.op('dve', lambda e: e.tensor_scalar(out=rec[mi][:], in0=aD[:, 0:QW], scalar1=1e-30, scalar2=None, op0=ALU.add), reads=[kD], writes=[('rec', mi)])
                T.op('dve', lambda e: e.reciprocal(out=rec[mi][:], in_=rec[mi][:]), reads=[('rec', mi)], writes=[('rec', mi)])
                T.op('dve', lambda e: e.tensor_tensor(out=mixo[mi][:], in0=aO[:, 0:QW], in1=rec[mi][:], op=ALU.mult), reads=[kO, ('rec', mi)], writes=[('mixo', mi)])
                T.dma(mixT_d[hidx, :, qt * QW:(qt + 1) * QW], mixo[mi][:], reads=[('mixo', mi)], writes=[('mixT_d', hidx, qt)])

    def v_proj(gsel):
        pend = None

        def fin(vb, c0, n, gi):
            pb = tr_i[0] % 2
            tr_i[0] += 1
            for s in range(n // 128):
                T.op('pe', lambda e: e.transpose(out=psT[pb][:, s * 128:(s + 1) * 128], in_=vtmp[vb][:, s * 128:(s + 1) * 128], identity=identb[:]),
                     reads=[('vtmp', vb), 'identb'], writes=[('psT', pb)])
            T.op('dve', lambda e: e.tensor_copy(out=V[:, c0 // 128:(c0 + n) // 128, :], in_=psT[pb][:, 0:n].rearrange("p (a b) -> p a b", b=128)),
                 reads=[('psT', pb)], writes=[('V', gi)])

        for gi in gsel:
            c0, n = groups[gi]
            pi = proj_mm(wvv, 'wvv', c0, n)
            if pend is not None:
                fin(*pend)
            vb = gi % 2
            T.op('act', lambda e: e.activation(out=vtmp[vb][:, 0:n], in_=psA[pi][:, 0:n], func=AF.Copy), reads=[('psA', pi)], writes=[('vtmp', vb)])
            pend = (vb, c0, n, gi)
        fin(*pend)

    groups = [(g * 512, 512) for g in range(8)] + [(4096, 256)]
    head_weights(0)
    n_heads = 16
    for hidx in range(n_heads):
        moba = hidx < 8
        proj_seq(wk, 'wk', [(c0, n, kT, ('kT', gi), c0) for gi, (c0, n) in enumerate(groups)], moba)
        if moba:
            gate_a(hidx)
        else:
            fox_bias(hidx)
        proj_seq(wq, 'wq', [(QT0 * 128 + qt * QW, QW, qT, ('qT', qt), qt * QW) for qt in range(3)], moba)
        v_proj(range(0, 5))
        if moba:
            gate_b(hidx)
        v_proj(range(5, 8))
        if moba:
            gate_c(hidx)
        v_proj(range(8, 9))
        if DEBUG:
            T.dma(dbg['kT'][hidx], kT[:], reads=[('kT', g) for g in range(9)])
            T.dma(dbg['V'][hidx], V[:].rearrange("p a b -> p (a b)"), reads=[('V', g) for g in range(9)])
            T.dma(dbg['qT'][hidx], qT[:], reads=[('qT', g) for g in range(3)])
        if hidx + 1 < n_heads:
            head_weights(hidx + 1)
        attention(hidx)
    T.barrier()
    es.close()
    es_A.close()
    if DEBUG:
        T.dma(dbg['mixT'], mixT_d, reads=[])
        T.barrier()

    esF = ExitStack()
    actT = S(esF, "actT", [128, KC, NQ], BF16, side="right")
    xn3T = actT
    esB = ExitStack()
    hres = S(esB, "hres", [128, NQS, D], F32)
    junk = S(esB, "junk", [128, D], BF16)
    xnb = [S(esB, "xnb%d" % i, [128, D], BF16) for i in range(2)]
    grep = S(esB, "grep", [128, D], F32)

    K2T = S(esB, "K2T", [128, 4, 256], BF16)
    V2 = S(esB, "V2", [128, 2, 512], BF16)
    wgrp0 = S(esB, "wgrp0", [128, KC, 512], BF16)
    es = ExitStack()
    memT = S(es, "memT", [128, KC, 256], BF16)
    xt = [S(es, "xt0", [128, D], F32)]
    wckv = S(es, "wckv", [128, KC, 1024], BF16)
    load_g(g_mem)
    load_w(wckv[:], 'wckv', wv(w_ckv), KC, 1024, engs=('act',))
    load_w(wgrp0[:], ('wgrp', 0), wv(w_o)[:, :, 0:512], KC, 512, engs=('act',))

    def _ldm(mt):
        return lambda: T.dma(xt[0][:], memx[mt * 128:(mt + 1) * 128, :], writes=[('xt', 0)])
    norm_seq([(_ldm(mt), xt[0][:], ('xt', 0), 'memT', mt * 128) for mt in range(2)], memT)
    T.dma(actT[:], mixT_d.rearrange("h p n -> p h n"), writes=[('actT', s) for s in range(NQS)])
    for s in range(NQS):
        T.dma(hres[:, s, :], xc[(QT0 + s) * 128:(QT0 + s + 1) * 128, :], writes=[('hres', s)])
    for h in range(4):
        pi = pa_i[0] % 2
        pa_i[0] += 1
        for kc in range(KC):
            T.op('pe', lambda e: e.matmul(psA[pi][:, 0:256], lhsT=wckv[:, kc, h * 128:(h + 1) * 128], rhs=memT[:, kc, :], start=(kc == 0), stop=(kc == KC - 1)),
                 reads=['wckv', 'memT'], writes=[('psA', pi)])
        T.op('act', lambda e: e.activation(out=K2T[:, h, :], in_=psA[pi][:, 0:256], func=AF.Copy), reads=[('psA', pi)], writes=['K2T'])
    for mt in range(2):
        pi = pa_i[0] % 2
        pa_i[0] += 1
        for kc in range(KC):
            T.op('pe', lambda e: e.matmul(psA[pi][:, :], lhsT=memT[:, kc, mt * 128:(mt + 1) * 128], rhs=wckv[:, kc, 512:1024], start=(kc == 0), stop=(kc == KC - 1)),
                 reads=['wckv', 'memT'], writes=[('psA', pi)])
        T.op('act', lambda e: e.activation(out=V2[:, mt, :], in_=psA[pi][:, :], func=AF.Copy), reads=[('psA', pi)], writes=['V2'])
    T.barrier()
    es.close()

    es = ExitStack()
    wcq = S(es, "wcq", [128, KC, 512], BF16)
    wco = S(es, "wco", [128, 4, D], BF16)
    es2 = ExitStack()
    wgrp = [wgrp0, S(es2, "wgrp1", [128, KC, 512], BF16)]
    load_g(g_x)
    npipe = NormPipe(actT)
    for cg in range(4):
        wb = cg % 2
        if cg + 1 < 4:
            load_w(wgrp[1 - wb][:], ('wgrp', 1 - wb), wv(w_o)[:, :, (cg + 1) * 512:(cg + 2) * 512], KC, 512, engs=('pool', 'act'))
        if cg == 2:
            load_w(wcq[:], 'wcq', wv(w_cq), KC, 512, engs=('pool', 'act'))
            load_w(wco[:], 'wco', wv(w_co), 4, D, engs=('pool', 'act'))
        for s in range(NQS):
            pi = pa_i[0] % 2
            pa_i[0] += 1
            for kc in range(KC):
                T.op('pe', lambda e: e.matmul(psA[pi][:, :], lhsT=actT[:, kc, s * 128:(s + 1) * 128], rhs=wgrp[wb][:, kc, :], start=(kc == 0), stop=(kc == KC - 1)),
                     reads=[('actT', s), ('wgrp', wb)], writes=[('psA', pi)])
            T.op('dve', lambda e: e.tensor_tensor(out=hres[:, s, cg * 512:(cg + 1) * 512], in0=psA[pi][:, :], in1=hres[:, s, cg * 512:(cg + 1) * 512], op=ALU.add),
                 reads=[('psA', pi), ('hres', s)], writes=[('hres', s)])
            if cg == 3 and not DEBUG:
                npipe.push(hres[:, s, :], ('hres', s), ('actT', s), s * 128)
    if DEBUG:
        T.dma(dbg['h1'].rearrange("s p n -> p s n"), hres[:], reads=[('hres', s) for s in range(NQS)])
        for s in range(NQS):
            npipe.push(hres[:, s, :], ('hres', s), ('actT', s), s * 128)
    npipe.flush()
    T.barrier()
    es2.close()

    xn2T = actT
    es2 = ExitStack()
    q2T = S(es2, "q2T", [128, 4, NQ], BF16)
    o2T = S(es2, "o2T", [128, 4, NQ], BF16)
    Pt = [S(es2, "Pt%d" % i, [128, QW], BF16) for i in range(3)]
    rec = S(es2, "rec", [128, QW], F32)
    load_g(g_ffn)
    for h in range(4):
        for qt in range(3):
            pi = pa_i[0] % 2
            pa_i[0] += 1
            for kc in range(KC):
                T.op('pe', lambda e: e.matmul(psA[pi][:, 0:QW], lhsT=wcq[:, kc, h * 128:(h + 1) * 128], rhs=xn2T[:, kc, qt * QW:(qt + 1) * QW], start=(kc == 0), stop=(kc == KC - 1)),
                     reads=['wcq'] + [('actT', 3 * qt + i) for i in range(3)], writes=[('psA', pi)])
            T.op('act', lambda e: e.activation(out=q2T[:, h, qt * QW:(qt + 1) * QW], in_=psA[pi][:, 0:QW], func=AF.Copy), reads=[('psA', pi)], writes=[('q2T', h, qt)])
    pt_i = 0
    for h in range(4):
        for qt in range(3):
            for mt in range(2):
                T.op('pe', lambda e: e.matmul(psS[mt][:, 0:QW], lhsT=K2T[:, h, mt * 128:(mt + 1) * 128], rhs=q2T[:, h, qt * QW:(qt + 1) * QW], start=True, stop=True),
                     reads=['K2T', ('q2T', h, qt)], writes=[('psS', mt)])
            pts = []
            for mt in range(2):
                pt = Pt[pt_i % 3]
                pk = ('Pt', pt_i % 3)
                pt_i += 1
                pts.append((pt, pk))
                T.op('act', lambda e: e.activation(out=pt[:], in_=psS[mt][:, 0:QW], func=AF.Exp, scale=SCALE), reads=[('psS', mt)], writes=[pk])
            for mt in range(2):
                pt, pk = pts[mt]
                T.op('pe', lambda e: e.matmul(psO[:, 0:QW], lhsT=V2[:, mt, h * 128:(h + 1) * 128], rhs=pt[:], start=(mt == 0), stop=(mt == 1)), reads=['V2', pk], writes=['psO'])
                T.op('pe', lambda e: e.matmul(psD[:, 0:QW], lhsT=onesb[:], rhs=pt[:], start=(mt == 0), stop=(mt == 1)), reads=['onesb', pk], writes=['psD'])
            T.op('dve', lambda e: e.reciprocal(out=rec[:], in_=psD[:, 0:QW]), reads=['psD'], writes=['rec'])
            T.op('dve', lambda e: e.tensor_tensor(out=o2T[:, h, qt * QW:(qt + 1) * QW], in0=psO[:, 0:QW], in1=rec[:], op=ALU.mult), reads=['psO', 'rec'], writes=[('o2T', qt)])
    npipe = NormPipe(actT)

    def _after(s):
        def f():
            if s == 0:
                T.op('dve', lambda e: e.tensor_scalar(out=actT[:, :, 0:128], in0=actT[:, :, 0:128], scalar1=halov[:, 0:1], scalar2=None, op0=ALU.mult),
                     reads=[('actT', 0), 'halov'], writes=[('actT', 0)])
            T.dma(hres_d[s], hres[:, s, :], reads=[('hres', s)], writes=[('hd', s, cg) for cg in range(4)])
        return f

    for s in range(NQS):
        for cg in range(4):
            pi = pa_i[0] % 2
            pa_i[0] += 1
            for kc in range(4):
                T.op('pe', lambda e: e.matmul(psA[pi][:, :], lhsT=o2T[:, kc, s * 128:(s + 1) * 128], rhs=wco[:, kc, cg * 512:(cg + 1) * 512], start=(kc == 0), stop=(kc == 3)),
                     reads=[('o2T', s // 3), 'wco'], writes=[('psA', pi)])
            T.op('dve', lambda e: e.tensor_tensor(out=hres[:, s, cg * 512:(cg + 1) * 512], in0=psA[pi][:, :], in1=hres[:, s, cg * 512:(cg + 1) * 512], op=ALU.add),
                 reads=[('psA', pi), ('hres', s)], writes=[('hres', s)])
        if not DEBUG:
            npipe.push(hres[:, s, :], ('hres', s), ('actT', s), s * 128, after=_after(s))
    if DEBUG:
        T.dma(dbg['h2'].rearrange("s p n -> p s n"), hres[:], reads=[('hres', s) for s in range(NQS)])
        for s in range(NQS):
            npipe.push(hres[:, s, :], ('hres', s), ('actT', s), s * 128, after=_after(s))
    npipe.flush()
    T.barrier()
    es2.close()
    es.close()
    esB.close()

    AKEYS = [('actT', s_) for s_ in range(NQS)]
    es = ExitStack()
    aT = S(es, "aT", [128, NFC, 1024], BF16)
    wd0 = S(es, "wd0", [128, NFC, 512], BF16)
    es2 = ExitStack()
    wgu = [S(es2, "wgu%d" % i, [128, KC, 256], BF16) for i in range(2)]
    convw = S(es2, "convw", [128, 88, 3], F32)
    convb = S(es2, "convb", [128, 88], F32)
    yg = [S(es2, "yg%d" % i, [128, 344], F32) for i in range(2)]
    yu = [S(es2, "yu%d" % i, [128, 344], F32) for i in range(2)]
    sg = [S(es2, "sg%d" % i, [128, 344], F32) for i in range(2)]
    T.dma(convw[:].rearrange("p a b -> p (a b)"), convw_d, writes=['convw'])
    T.dma(convb[:], convb_d, writes=['convb'])
    it = 0
    def load_gu(c):
        wb = c % 2
        load_w(wgu[wb][:, :, 0:128], ('wgu', wb), wv(w_up)[:, :, c * 128:(c + 1) * 128], KC, 128, engs=('pool',))
        load_w(wgu[wb][:, :, 128:256], ('wgu', wb), wv(w_up)[:, :, DFF + c * 128:DFF + (c + 1) * 128], KC, 128, engs=('act',))

    load_gu(0)
    for c in range(NFC):
        wb = c % 2
        for tt in range(3):
            o0 = 342 * tt
            o1 = min(1024, 342 * (tt + 1))
            no = o1 - o0
            n = no + 2
            c0 = 128 + o0 - 2
            pg, pu = (psA[0], psA[1]) if it % 2 == 0 else (psS[0], psS[1])
            kg, ku = (('psA', 0), ('psA', 1)) if it % 2 == 0 else (('psS', 0), ('psS', 1))
            b = it % 2
            it += 1
            for kc in range(KC):
                T.op('pe', lambda e: e.matmul(pg[:, 0:n], lhsT=wgu[wb][:, kc, 0:128], rhs=xn3T[:, kc, c0:c0 + n], start=(kc == 0), stop=(kc == KC - 1)),
                     reads=[('wgu', wb)] + AKEYS, writes=[kg])
            for kc in range(KC):
                T.op('pe', lambda e: e.matmul(pu[:, 0:n], lhsT=wgu[wb][:, kc, 128:256], rhs=xn3T[:, kc, c0:c0 + n], start=(kc == 0), stop=(kc == KC - 1)),
                     reads=[('wgu', wb)] + AKEYS, writes=[ku])
            for (pp, pkey, yy, ykey, ch) in [(pg, kg, yg[b], ('yg', b), c), (pu, ku, yu[b], ('yu', b), NFC + c)]:
                T.op('act', lambda e: e.activation(out=yy[:, 0:no], in_=pp[:, 2:n], func=AF.Identity, scale=convw[:, ch, 2:3], bias=convb[:, ch:ch + 1]),
                     reads=[pkey, 'convw', 'convb'], writes=[ykey])
                T.op('dve', lambda e: e.scalar_tensor_tensor(out=yy[:, 0:no], in0=pp[:, 1:n - 1], scalar=convw[:, ch, 1:2], in1=yy[:, 0:no], op0=ALU.mult, op1=ALU.add),
                     reads=[pkey, 'convw', ykey], writes=[ykey])
                T.op('dve', lambda e: e.scalar_tensor_tensor(out=yy[:, 0:no], in0=pp[:, 0:n - 2], scalar=convw[:, ch, 0:1], in1=yy[:, 0:no], op0=ALU.mult, op1=ALU.add),
                     reads=[pkey, 'convw', ykey], writes=[ykey])
            T.op('act', lambda e: e.activation(out=sg[b][:, 0:no], in_=yg[b][:, 0:no], func=AF.Silu), reads=[('yg', b)], writes=[('sg', b)])
            T.op('dve', lambda e: e.tensor_tensor(out=aT[:, c, o0:o1], in0=sg[b][:, 0:no], in1=yu[b][:, 0:no], op=ALU.mult),
                 reads=[('sg', b), ('yu', b)], writes=[('aT', c)])
            if tt == 0 and c + 1 < NFC:
                load_gu(c + 1)
            if tt == 1 and NFC - 14 <= c < NFC - 3:
                k_ = c - (NFC - 14)
                load_w(wd0[:, 4 * k_:4 * k_ + 4, :], ('wd', 0), wv(w_down)[:, 4 * k_:4 * k_ + 4, 0:512], 4, 512, engs=('pool',))
    T.barrier()
    es2.close()
    esF.close()

    es2 = ExitStack()
    wd = [wd0, S(es2, "wd1", [128, NFC, 512], BF16)]
    h2t = [S(es2, "h2t%d" % i, [128, 512], F32) for i in range(8)]
    for cg in range(4):
        wb = cg % 2
        for s in range(8):
            T.dma(h2t[s][:], hres_d[1 + s, :, cg * 512:(cg + 1) * 512], reads=[('hd', 1 + s, cg)], writes=[('h2t', s)])
        if cg + 1 < 4:
            load_w(wd[1 - wb][:], ('wd', 1 - wb), wv(w_down)[:, :, (cg + 1) * 512:(cg + 2) * 512], NFC, 512, engs=('act', 'act', 'pool'))
        for s in range(8):
            pi = pa_i[0] % 2
            pa_i[0] += 1
            for fc in range(NFC):
                T.op('pe', lambda e: e.matmul(psA[pi][:, :], lhsT=aT[:, fc, s * 128:(s + 1) * 128], rhs=wd[wb][:, fc, :], start=(fc == 0), stop=(fc == NFC - 1)),
                     reads=[('aT', fc), ('wd', wb)], writes=[('psA', pi)])
            T.op('dve', lambda e: e.tensor_tensor(out=h2t[s][:], in0=psA[pi][:, :], in1=h2t[s][:], op=ALU.add), reads=[('psA', pi), ('h2t', s)], writes=[('h2t', s)])
            T.dma(hres_d[1 + s, :, cg * 512:(cg + 1) * 512], h2t[s][:], reads=[('h2t', s)], writes=[('hd', 1 + s, cg)])
    T.barrier()
    es2.close()
    es.close()
    if DEBUG:
        T.dma(dbg['h3'], hres_d, reads=[])
        T.barrier()

    es = ExitStack()
    xt = [S(es, "xt%d" % i, [128, D], F32) for i in range(3)]
    yo = [S(es, "yo%d" % i, [128, D], F32) for i in range(2)]
    junk = S(es, "junk", [128, D], BF16)
    grep = S(es, "grep", [128, D], F32)
    load_g(g_fin)

    def _ldf(s_):
        T.dma(xt[s_ % 3][:], hres_d[1 + s_], reads=[('hd', 1 + s_, cg) for cg in range(4)], writes=[('xt', s_ % 3)])
    _ldf(0)
    _ldf(1)
    for s in range(8):
        b = s % 2
        xb_ = s % 3
        rs, sk = rms_stats(xt[xb_][:], ('xt', xb_))
        T.op('dve', lambda e: e.scalar_tensor_tensor(out=yo[b][:], in0=xt[xb_][:], scalar=rs, in1=grep[:], op0=ALU.mult, op1=ALU.mult),
             reads=[('xt', xb_), sk, 'grep'], writes=[('yo', b)])
        if s + 2 < 8:
            _ldf(s + 2)
        T.dma(y[s * 128:(s + 1) * 128, :], yo[b][:], reads=[('yo', b)])
    T.barrier(engines=('sp',))
    T.barrier(engines=('pe', 'act', 'dve', 'pool'))
    es.close()
    es_top.close()
    return nc, T


_CACHE = {}


def _rope_tables(pos):
    half = 64
    inv = (np.float32(10000.0) ** (-(np.arange(half, dtype=np.float32)) / np.float32(half))).astype(np.float32)
    ang = (pos.astype(np.float32)[None, :] * inv[:, None]).astype(np.float32)
    c = np.cos(ang).astype(np.float32)
    s = np.sin(ang).astype(np.float32)
    cosT = np.concatenate([c, c], axis=0)
    sinT = np.concatenate([-s, s], axis=0)
    return np.ascontiguousarray(cosT), np.ascontiguousarray(sinT)


def _core_inputs(j, xb, memb, shared):
    others = [i for i in range(4) if i != j]
    xo = [xb[i * 1024:(i + 1) * 1024] for i in others]
    if j > 0:
        halo = xb[j * 1024 - 256:j * 1024]
        halo_pos = np.arange(j * 1024 - 256, j * 1024)
    else:
        halo = np.zeros((256, D), np.float32)
        halo_pos = np.arange(256)
    own = xb[j * 1024:(j + 1) * 1024]
    xcx = np.ascontiguousarray(np.concatenate(xo + [halo, own], axis=0))
    pos = np.concatenate([np.arange(i * 1024, (i + 1) * 1024) for i in others] + [halo_pos, np.arange(j * 1024, (j + 1) * 1024)])
    cosT, sinT = _rope_tables(pos)
    valid = np.zeros(NCT, np.float32)
    for kt in range(24):
        i = others[kt // 8]
        valid[kt] = 1.0 if (i < j and not (i == j - 1 and kt % 8 >= 6)) else 0.0
    valid[24:26] = 1.0 if j > 0 else 0.0
    valid[26:] = 1.0
    kbias = np.tile(((1.0 - valid) * NEGM)[None, :], (128, 1)).astype(np.float32)
    lfvalid = np.tile(valid[None, :], (8, 1)).astype(np.float32)
    halov = np.full((128, 1), 1.0 if j > 0 else 0.0, np.float32)
    past = np.zeros((NQS, NBLK), np.float32)
    ownm = np.zeros((NQS, NBLK), np.float32)
    for s in range(NQS):
        tb = (QT0 + s) // 2
        ownm[s, tb] = 1.0
        for b in range(NBLK):
            if b < 12:
                i = others[b // 4]
                ok = (i < j) and not (i == j - 1 and b % 4 == 3)
            elif b == 12:
                ok = (j > 0) and tb > 12
            else:
                ok = b < tb
            past[s, b] = 1.0 if ok else 0.0
    pastbias = ((1.0 - past) * -1e30).astype(np.float32)
    rep = lambda a: np.ascontiguousarray(np.tile(a.reshape(1, -1), (128, 1)).astype(np.float32))
    m = dict(shared)
    m.update({"xc": xcx, "memx": np.ascontiguousarray(memb), "cosT": cosT, "sinT": sinT, "kbias": kbias, "lfvalid": lfvalid,
              "halov": halov, "pastbias": rep(pastbias), "past01": rep(past), "own01": rep(ownm)})
    return m


def kernel(x, mem, attn_norm_g, w_in, b_f, w_o, xattn_norm_g, mem_norm_g, w_cq, w_ckv, w_co, ffn_norm_g,
           w_up, conv_w, conv_b, w_down, final_norm_g):
    f = lambda a: np.ascontiguousarray(np.asarray(a, dtype=np.float32))
    x = f(x); mem = f(mem)
    esel = np.zeros((128, NBLK, 128), np.float32)
    for b in range(NBLK):
        esel[b, b, :] = 1.0
    rp = np.zeros((128, 128), np.float32)
    for m_ in range(128):
        rp[(m_ + 64) % 128, m_] = 1.0
    tri = (np.arange(128)[:, None] <= np.arange(128)[None, :]).astype(np.float32)
    cw = f(conv_w)[0]
    shared = {
        "g_attn": f(attn_norm_g).reshape(1, D), "g_x": f(xattn_norm_g).reshape(1, D), "g_mem": f(mem_norm_g).reshape(1, D),
        "g_ffn": f(ffn_norm_g).reshape(1, D), "g_fin": f(final_norm_g).reshape(1, D),
        "w_in": f(w_in)[0], "w_o": f(w_o)[0], "w_cq": f(w_cq)[0], "w_ckv": f(w_ckv)[0], "w_co": f(w_co)[0],
        "w_up": f(w_up)[0], "w_down": f(w_down)[0],
        "convw": np.ascontiguousarray(cw.reshape(3, 88, 128).transpose(2, 1, 0).reshape(128, 88 * 3)),
        "convb": np.ascontiguousarray(f(conv_b)[0].reshape(88, 128).T),
        "bf": f(b_f).reshape(8, 1),
        "identb": np.eye(128, dtype=np.float32).astype(NPBF), "rperm": rp.astype(NPBF), "onesb": np.ones((128, 128), NPBF),
        "tri": tri.astype(NPBF), "esel": esel.reshape(128, NBLK * 128).astype(NPBF),
        "ident8": np.eye(8, dtype=np.float32), "ones8": np.ones((8, 128), np.float32),
    }
    if 'nc' not in _CACHE:
        _CACHE['nc'] = build_program()
    nc, T = _CACHE['nc']
    in_maps = []
    for c in range(8):
        b, j = c // 4, c % 4
        in_maps.append(_core_inputs(j, x[b], mem[b], shared))
    res = run_bass_kernel_spmd(nc, in_maps, core_ids=list(range(8)))
    _CACHE['last'] = res
    out = np.zeros((2, 4096, D), np.float32)
    for c in range(8):
        b, j = c // 4, c % 4
        out[b, j * 1024:(j + 1) * 1024] = res.results[c]["y"]
    return out
```

```python
import os
import numpy as np
import ml_dtypes
import concourse.bass as bass
import concourse.mybir as mybir
from concourse.bass_utils import run_bass_kernel_spmd
from contextlib import ExitStack

F32 = mybir.dt.float32
BF16 = mybir.dt.bfloat16
AF = mybir.ActivationFunctionType
ALU = mybir.AluOpType
AX = mybir.AxisListType
NPBF = ml_dtypes.bfloat16

D = 2048
KC = 16
NCT = 34
NCTX = NCT * 128
QT0 = 25
NQS = 9
NQ = NQS * 128
QW = 384
NBLK = 17
SCALE = 1.0 / float(np.sqrt(128.0))
NEGM = -30000.0
DFF = 5632
NFC = 44
DEBUG = bool(int(os.environ.get("MK_DEBUG", "0")))
STOP_AFTER = os.environ.get("MK_STOP", "")


class Tracker:
    def __init__(self, nc, n_dma_sems=28):
        self.nc = nc
        self.eng = {'pe': nc.tensor, 'act': nc.scalar, 'dve': nc.vector, 'pool': nc.gpsimd, 'sp': nc.sync}
        self.sem = {k: nc.alloc_semaphore('sem_' + k) for k in ['pe', 'act', 'dve', 'pool']}
        self.cnt = {k: 0 for k in self.sem}
        self.dma_sems = [nc.alloc_semaphore('dsem%d' % i) for i in range(n_dma_sems)]
        self.dma_cnt = [0] * n_dma_sems
        self.dma_rr = 0
        self.known = {}
        self.lastw = {}
        self.readers = {}
        self.semobj = dict(self.sem)
        for i, s in enumerate(self.dma_sems):
            self.semobj[('d', i)] = s
        self.n_ops = 0

    def _wait(self, e, semkey, val):
        if semkey == 'pe' and e == 'pe':
            return
        kk = (e, semkey)
        if self.known.get(kk, 0) >= val:
            return
        self.known[kk] = val
        self.eng[e].wait_ge(self.semobj[semkey], val)

    def _deps(self, e, reads, writes):
        for r in reads:
            w = self.lastw.get(r)
            if w is not None:
                self._wait(e, *w)
        for r in writes:
            w = self.lastw.get(r)
            if w is not None:
                self._wait(e, *w)
            rd = self.readers.get(r)
            if rd:
                for sk, v in rd.items():
                    self._wait(e, sk, v)

    def _record(self, tok, reads, writes):
        sk, v = tok
        for r in reads:
            d = self.readers.setdefault(r, {})
            if d.get(sk, 0) < v:
                d[sk] = v
        for r in writes:
            self.lastw[r] = tok
            self.readers[r] = {}

    def op(self, e, fn, reads=(), writes=()):
        self._deps(e, reads, writes)
        ins = fn(self.eng[e])
        self.cnt[e] += 1
        ins.then_inc(self.sem[e], 1)
        tok = (e, self.cnt[e])
        self._record(tok, reads, writes)
        self.n_ops += 1
        return tok

    def dma(self, out, in_, reads=(), writes=(), q='sp'):
        self._deps(q, reads, writes)
        i = self.dma_rr
        self.dma_rr = (self.dma_rr + 1) % len(self.dma_sems)
        if self.dma_cnt[i] > 0:
            self._wait(q, ('d', i), self.dma_cnt[i])
        self.dma_cnt[i] += 16
        self.eng[q].dma_start(out=out, in_=in_).then_inc(self.dma_sems[i], 16)
        tok = (('d', i), self.dma_cnt[i])
        self._record(tok, reads, writes)
        self.n_ops += 1
        return tok

    def barrier(self, engines=('pe', 'act', 'dve', 'pool', 'sp')):
        toks = [(k, c) for k, c in self.cnt.items() if c > 0]
        toks += [(('d', i), c) for i, c in enumerate(self.dma_cnt) if c > 0]
        for e in engines:
            for t in toks:
                if t[0] == e:
                    continue
                self._wait(e, *t)
        self.lastw = {}
        self.readers = {}


def build_program():
    nc = bass.Bass("TRN2", target_bir_lowering=False, dynamic_dma_scratch_size=512)
    T = Tracker(nc)

    def din(name, shape, dt=F32):
        return nc.dram_tensor(name, list(shape), dt, kind="ExternalInput").ap()

    xc = din("xc", [NCTX, D])
    memx = din("memx", [256, D])
    g_attn = din("g_attn", [1, D]); g_x = din("g_x", [1, D]); g_mem = din("g_mem", [1, D])
    g_ffn = din("g_ffn", [1, D]); g_fin = din("g_fin", [1, D])
    w_in = din("w_in", [D, 6152]); w_o = din("w_o", [D, D]); w_cq = din("w_cq", [D, 512])
    w_ckv = din("w_ckv", [D, 1024]); w_co = din("w_co", [512, D]); w_up = din("w_up", [D, 2 * DFF])
    w_down = din("w_down", [DFF, D])
    convw_d = din("convw", [128, 88 * 3]); convb_d = din("convb", [128, 88]); bf_d = din("bf", [8, 1])
    cos_d = din("cosT", [128, NCTX]); sin_d = din("sinT", [128, NCTX])
    kbias_d = din("kbias", [128, NCT]); lfv_d = din("lfvalid", [8, NCT]); halov_d = din("halov", [128, 1])
    pastb_d = din("pastbias", [128, NQS * NBLK]); past01_d = din("past01", [128, NQS * NBLK]); own01_d = din("own01", [128, NQS * NBLK])
    identb_d = din("identb", [128, 128], BF16); rperm_d = din("rperm", [128, 128], BF16)
    onesb_d = din("onesb", [128, 128], BF16); tri_d = din("tri", [128, 128], BF16)
    esel_d = din("esel", [128, NBLK * 128], BF16); ident8_d = din("ident8", [8, 8]); ones8_d = din("ones8", [8, 128])
    y = nc.dram_tensor("y", [1024, D], F32, kind="ExternalOutput").ap()
    mixT_d = nc.dram_tensor("mixT_d", [16, 128, NQ], BF16, kind="Internal").ap()
    hres_d = nc.dram_tensor("hres_d", [NQS, 128, D], F32, kind="Internal").ap()
    dbg = {}
    if DEBUG:
        dbg['xnT'] = nc.dram_tensor("dbg_xnT", [128, KC, NCTX], BF16, kind="ExternalOutput").ap()
        dbg['ck'] = nc.dram_tensor("dbg_ck", [128, NCT * 8], F32, kind="ExternalOutput").ap()
        dbg['kT'] = nc.dram_tensor("dbg_kT", [16, 128, NCTX], BF16, kind="ExternalOutput").ap()
        dbg['V'] = nc.dram_tensor("dbg_V", [16, 128, NCT * 128], BF16, kind="ExternalOutput").ap()
        dbg['qT'] = nc.dram_tensor("dbg_qT", [16, 128, NQ], BF16, kind="ExternalOutput").ap()
        dbg['mixT'] = nc.dram_tensor("dbg_mixT", [16, 128, NQ], BF16, kind="ExternalOutput").ap()
        dbg['h1'] = nc.dram_tensor("dbg_h1", [NQS, 128, D], F32, kind="ExternalOutput").ap()
        dbg['h2'] = nc.dram_tensor("dbg_h2", [NQS, 128, D], F32, kind="ExternalOutput").ap()
        dbg['h3'] = nc.dram_tensor("dbg_h3", [NQS, 128, D], F32, kind="ExternalOutput").ap()

    wv = lambda ap: ap.rearrange("(c p) n -> p c n", p=128)

    psA = [nc.alloc_psum_tensor("psA%d" % i, [128, 512], F32) for i in range(2)]
    psS = [nc.alloc_psum_tensor("psS%d" % i, [128, 512], F32) for i in range(2)]
    psTf = [nc.alloc_psum_tensor("psT%d" % i, [128, 512], F32) for i in range(2)]

    class _BV:
        def __init__(self, t):
            self.t = t

        def __getitem__(self, key):
            return self.t[:].bitcast(BF16)[key]
    psT = [_BV(t) for t in psTf]
    psO = nc.alloc_psum_tensor("psO", [128, 512], F32)
    psD = nc.alloc_psum_tensor("psD", [128, 512], F32)

    A = lambda name, shape, dt: nc.alloc_sbuf_tensor("sb_" + name, shape, dt)
    identb = A("identb", [128, 128], BF16); rperm = A("rperm", [128, 128], BF16)
    onesb = A("onesb", [128, 128], BF16); tri = A("tri", [128, 128], BF16)
    ident8 = A("ident8", [8, 8], F32); ones8 = A("ones8", [8, 128], F32)
    halov = A("halov", [128, 1], F32)
    for t_, d_, nm in [(identb, identb_d, 'identb'), (rperm, rperm_d, 'rperm'), (onesb, onesb_d, 'onesb'), (tri, tri_d, 'tri'),
                       (ident8, ident8_d, 'ident8'), (ones8, ones8_d, 'ones8'), (halov, halov_d, 'halov')]:
        T.dma(t_[:], d_, writes=[nm])
    wst = [A("wst%d" % i, [128, 2048], F32) for i in range(2)]
    wst_i = [0]
    stat = A("stat", [128, 2 * (NCT + 2 * NQS + 12)], F32)
    stat_i = [0]

    def load_w(dst, dkey, src, G, C, engs=('pool',)):
        gs = max(1, 2048 // C)
        g0 = 0
        k = 0
        while g0 < G:
            g1 = min(G, g0 + gs)
            i = wst_i[0] % 2
            wst_i[0] += 1
            st = wst[i][:, 0:(g1 - g0) * C].rearrange("p (g c) -> p g c", c=C)
            T.dma(st, src[:, g0:g1, :], writes=[('wst', i)])
            en = engs[k % len(engs)]
            k += 1
            if en == 'act':
                T.op('act', lambda e: e.activation(out=dst[:, g0:g1, :], in_=st, func=AF.Copy), reads=[('wst', i)], writes=[dkey])
            else:
                T.op(en, lambda e: e.tensor_copy(out=dst[:, g0:g1, :], in_=st), reads=[('wst', i)], writes=[dkey])
            g0 = g1

    def rms_sq(src, skey):
        k = stat_i[0]
        stat_i[0] += 1
        ss = stat[:, 2 * k:2 * k + 1]
        rs = stat[:, 2 * k + 1:2 * k + 2]
        sk = ('stat', k)
        T.op('act', lambda e: e.activation(out=junk[:], in_=src, func=AF.Square, accum_out=ss), reads=[skey], writes=['junk', sk])
        return ss, rs, sk

    def rms_fin(ss, rs, sk):
        T.op('dve', lambda e: e.tensor_scalar(out=rs, in0=ss, scalar1=1.0 / D, scalar2=1e-6, op0=ALU.mult, op1=ALU.add), reads=[sk], writes=[sk])
        T.op('act', lambda e: e.activation(out=rs, in_=rs, func=AF.Sqrt), reads=[sk], writes=[sk])
        T.op('dve', lambda e: e.reciprocal(out=rs, in_=rs), reads=[sk], writes=[sk])

    def rms_stats(src, skey):
        ss, rs, sk = rms_sq(src, skey)
        rms_fin(ss, rs, sk)
        return rs, sk

    tr_i = [0]

    def norm_mul(src, skey, rs, sk):
        b = tr_i[0] % 2
        xb = xnb[b]
        T.op('dve', lambda e: e.scalar_tensor_tensor(out=xb[:], in0=src, scalar=rs, in1=grep[:], op0=ALU.mult, op1=ALU.mult),
             reads=[skey, sk, 'grep'], writes=[('xnb', b)])
        return b

    def norm_tr(b, dstT, dkey, col):
        xb = xnb[b]
        for half in range(2):
            pb = tr_i[0] % 2
            tr_i[0] += 1
            pt = psT[pb]
            for k in range(8):
                kc = half * 8 + k
                T.op('pe', lambda e: e.transpose(out=pt[:, k * 128:(k + 1) * 128], in_=xb[:, kc * 128:(kc + 1) * 128], identity=identb[:]),
                     reads=[('xnb', b), 'identb'], writes=[('psT', pb)])
            src_v = pt[:, :].rearrange("p (a b) -> p a b", b=128)
            dst_v = dstT[:, half * 8:(half + 1) * 8, col:col + 128]
            T.op('act', lambda e: e.activation(out=dst_v, in_=src_v, func=AF.Copy), reads=[('psT', pb)], writes=[dkey])

    def norm_seq(items, dstT):
        pend = None
        for (load_fn, src, skey, dkey, col) in items:
            if pend is not None:
                b = norm_mul(pend[0], pend[1], pend[2], pend[3])
            if load_fn is not None:
                load_fn()
            ss, rs, sk = rms_sq(src, skey)
            if pend is not None:
                norm_tr(b, dstT, pend[4], pend[5])
            rms_fin(ss, rs, sk)
            pend = (src, skey, rs, sk, dkey, col)
        b = norm_mul(pend[0], pend[1], pend[2], pend[3])
        norm_tr(b, dstT, pend[4], pend[5])

    class NormPipe:
        def __init__(self, dstT):
            self.dstT = dstT
            self.statted = None
            self.mulled = None

        def _step(self, new):
            m = self.mulled
            if m is not None:
                norm_tr(m[0], self.dstT, m[1], m[2])
                if m[3] is not None:
                    m[3]()
                self.mulled = None
            p = self.statted
            if p is not None:
                b = norm_mul(p[0], p[1], p[2], p[3])
                self.mulled = (b, p[4], p[5], p[6])
                self.statted = None
            if new is not None:
                src, skey, dkey, col, after = new
                ss, rs, sk = rms_sq(src, skey)
                rms_fin(ss, rs, sk)
                self.statted = (src, skey, rs, sk, dkey, col, after)

        def push(self, src, skey, dkey, col, after=None):
            self._step((src, skey, dkey, col, after))

        def flush(self):
            self._step(None)
            self._step(None)

    def load_g(g_d):
        T.dma(grep[:], g_d.partition_broadcast(128), writes=['grep'])

    es_top = ExitStack()
    _uid = [0]

    def S(st, name, shape, dt, side=None):
        _uid[0] += 1
        return st.enter_context(nc.sbuf_tensor("sb%d_%s" % (_uid[0], name), list(shape), dt, side=side))

    es_A = ExitStack()
    xnT = S(es_A, "xnT", [128, KC, NCTX], BF16)
    ck = S(es_A, "ck", [128, NCT, 8], F32)
    ckk = S(es_A, "ckk", [128, NCT, 8], F32)
    cref = S(es_A, "cref", [128, 24], F32)
    kbias = S(es_A, "kbias", [128, NCT], F32)
    wzf = S(es_A, "wzf", [128, KC, 8], F32)
    wzb = S(es_A, "wzb", [128, KC, 8], BF16)
    bfn = S(es_A, "bfn", [8, 1], F32)
    lfv = S(es_A, "lfv", [8, NCT], F32)
    one8 = S(es_A, "one8", [8, 1], F32)
    rhsd = S(es_A, "rhsd", [8, 8, 3], F32)
    T.dma(kbias[:], kbias_d, writes=['kbias'])

    es = ExitStack()
    xt = [S(es, "xt%d" % i, [128, D], F32) for i in range(3)]
    junk = S(es, "junk", [128, D], BF16)
    xnb = [S(es, "xnb%d" % i, [128, D], BF16) for i in range(2)]
    grep = S(es, "grep", [128, D], F32)
    load_g(g_attn)
    T.dma(wzf[:], wv(w_in)[:, :, 6144:6152], writes=['wzf'])
    T.dma(bfn[:], bf_d, writes=['bfn'])
    T.dma(lfv[:], lfv_d, writes=['lfv'])
    npipe0 = NormPipe(xnT)
    for tl in range(NCT):
        b = tl % 3
        T.dma(xt[b][:], xc[tl * 128:(tl + 1) * 128, :], writes=[('xt', b)])
        npipe0.push(xt[b][:], ('xt', b), ('xnT', tl), tl * 128)
    npipe0.flush()
    if DEBUG:
        T.dma(dbg['xnT'], xnT[:], reads=[('xnT', tl) for tl in range(NCT)])

    T.barrier()
    es.close()
    es = ExitStack()
    zl = S(es, "zl", [8, NCTX], F32)
    cT = S(es, "cT", [8, NCTX], F32)
    T.op('pool', lambda e: e.tensor_copy(out=wzb[:], in_=wzf[:]), reads=['wzf'], writes=['wzb'])
    T.op('dve', lambda e: e.tensor_scalar(out=bfn[:], in0=bfn[:], scalar1=-1.0, scalar2=None, op0=ALU.mult), reads=['bfn'], writes=['bfn'])
    T.op('dve', lambda e: e.memset(one8[:], 1.0), writes=['one8'])
    groups = [(g * 512, 512) for g in range(8)] + [(4096, 256)]
    for gi, (c0, n) in enumerate(groups):
        pa = psA[gi % 2]
        for kc in range(KC):
            T.op('pe', lambda e: e.matmul(pa[0:8, 0:n], lhsT=wzb[:, kc, :], rhs=xnT[:, kc, c0:c0 + n], start=(kc == 0), stop=(kc == KC - 1)),
                 reads=['wzb'] + [('xnT', tl) for tl in range(c0 // 128, (c0 + n) // 128)], writes=[('psA', gi % 2)])
        T.op('act', lambda e: e.activation(out=zl[:, c0:c0 + n], in_=pa[0:8, 0:n], func=AF.Exp, scale=-1.0, bias=bfn[:]),
             reads=[('psA', gi % 2), 'bfn'], writes=['zl'])
    T.op('act', lambda e: e.activation(out=zl[:], in_=zl[:], func=AF.Ln, bias=1.0), reads=['zl'], writes=['zl'])
    T.op('dve', lambda e: e.scalar_tensor_tensor(out=zl[:].rearrange("p (a b) -> p a b", b=128), in0=zl[:].rearrange("p (a b) -> p a b", b=128),
                                                 scalar=-1.0, in1=lfv[:].unsqueeze(2).to_broadcast([8, NCT, 128]), op0=ALU.mult, op1=ALU.mult),
         reads=['zl', 'lfv'], writes=['zl'])
    T.op('dve', lambda e: e.tensor_tensor_scan(out=cT[:], data0=one8[:].to_broadcast([8, NCTX]), data1=zl[:], initial=0.0, op0=ALU.mult, op1=ALU.add),
         reads=['zl', 'one8'], writes=['cT'])
    for tl in range(NCT):
        T.op('pe', lambda e: e.transpose(out=psS[0][:, tl * 8:(tl + 1) * 8], in_=cT[:, tl * 128:(tl + 1) * 128], identity=ident8[:]),
             reads=['cT', 'ident8'], writes=[('psS', 0)])
    T.op('dve', lambda e: e.tensor_copy(out=ck[:], in_=psS[0][:, 0:NCT * 8].rearrange("p (a b) -> p a b", b=8)), reads=[('psS', 0)], writes=['ck'])
    T.op('dve', lambda e: e.tensor_tensor(out=ckk[:], in0=ck[:], in1=kbias[:].unsqueeze(2).to_broadcast([128, NCT, 8]), op=ALU.subtract),
         reads=['ck', 'kbias'], writes=['ckk'])
    for qt in range(3):
        tm = QT0 * 128 + qt * QW + QW // 2
        T.op('dve', lambda e: e.tensor_scalar(out=rhsd[:, :, qt], in0=ident8[:], scalar1=cT[:, tm:tm + 1], scalar2=None, op0=ALU.mult),
             reads=['cT', 'ident8'], writes=['rhsd'])
    T.op('pe', lambda e: e.matmul(psS[1][:, 0:24], lhsT=ones8[:], rhs=rhsd[:].rearrange("p a b -> p (a b)"), start=True, stop=True),
         reads=['ones8', 'rhsd'], writes=[('psS', 1)])
    T.op('dve', lambda e: e.tensor_copy(out=cref[:], in_=psS[1][:, 0:24]), reads=[('psS', 1)], writes=['cref'])
    if DEBUG:
        T.dma(dbg['ck'], ck[:].rearrange("p a b -> p (a b)"), reads=['ck'])
    T.barrier()
    es.close()

    es = ExitStack()
    wq = S(es, "wq", [128, KC, 128], BF16); wk = S(es, "wk", [128, KC, 128], BF16); wvv = S(es, "wvv", [128, KC, 128], BF16)
    kT = S(es, "kT", [128, NCTX], BF16)
    V = S(es, "V", [128, NCT, 128], BF16)
    qT = S(es, "qT", [128, NQ], BF16)
    vtmp = [S(es, "vtmp%d" % i, [128, 512], BF16) for i in range(2)]
    kbf = [S(es, "kbf%d" % i, [128, 512], BF16) for i in range(2)]
    cs = [S(es, "cs%d" % i, [128, 512], F32) for i in range(2)]
    sn = [S(es, "sn%d" % i, [128, 512], F32) for i in range(2)]
    Pt = [S(es, "Pt%d" % i, [128, QW], BF16) for i in range(3)]
    maskT = S(es, "maskT", [128, NQ], BF16)
    T.op('pool', lambda e: e.memset(maskT[:], 0.0), writes=[('maskT', 0), ('maskT', 1), ('maskT', 2)])
    fb = S(es, "fb", [128, 3, NCT], F32)
    rec = [S(es, "rec%d" % i, [128, QW], F32) for i in range(2)]
    mixo = [S(es, "mixo%d" % i, [128, QW], BF16) for i in range(2)]
    esel = S(es, "esel", [128, NBLK * 128], BF16)
    pastb = S(es, "pastb", [128, NQS * NBLK], F32); past01 = S(es, "past01", [128, NQS * NBLK], F32); own01 = S(es, "own01", [128, NQS * NBLK], F32)
    ksum = S(es, "ksum", [128, NBLK], F32); ksumb = S(es, "ksumb", [128, NBLK], BF16)
    gm = S(es, "gm", [128, NQS * NBLK], F32); m8 = S(es, "m8", [128, NQS * 8], F32); ge = S(es, "ge", [128, NQS * NBLK], F32)
    maskv = S(es, "maskv", [128, NQS, 32], BF16)
    T.op('dve', lambda e: e.memset(maskv[:], 0.0), writes=['maskv'])
    for t_, d_, nm in [(esel, esel_d, 'esel'), (pastb, pastb_d, 'pastb'), (past01, past01_d, 'past01'), (own01, own01_d, 'own01')]:
        T.dma(t_[:], d_, writes=[nm])

    rope_i = [0]
    pa_i = [0]

    def proj_mm(wt, wkey, c0, n):
        pi = pa_i[0] % 2
        pa_i[0] += 1
        pa = psA[pi]
        xkeys = [('xnT', tl) for tl in range(c0 // 128, (c0 + n) // 128)]
        for kc in range(KC):
            T.op('pe', lambda e: e.matmul(pa[:, 0:n], lhsT=wt[:, kc, :], rhs=xnT[:, kc, c0:c0 + n], start=(kc == 0), stop=(kc == KC - 1)),
                 reads=[wkey] + xkeys, writes=[('psA', pi)])
        return pi

    def proj_fin(pi, c0, n, moba, dst, dkey, dcol):
        pa = psA[pi]
        if not moba:
            T.op('act', lambda e: e.activation(out=dst[:, dcol:dcol + n], in_=pa[:, 0:n], func=AF.Copy), reads=[('psA', pi)], writes=[dkey])
            return
        r = rope_i[0] % 2
        rope_i[0] += 1
        T.dma(cs[r][:, 0:n], cos_d[:, c0:c0 + n], writes=[('cs', r)])
        T.dma(sn[r][:, 0:n], sin_d[:, c0:c0 + n], writes=[('sn', r)])
        T.op('act', lambda e: e.activation(out=kbf[r][:, 0:n], in_=pa[:, 0:n], func=AF.Copy), reads=[('psA', pi)], writes=[('kbf', r)])
        return r

    def rope_fin(r, n, dst, dkey, dcol):
        T.op('pe', lambda e: e.matmul(psS[r][:, 0:n], lhsT=rperm[:], rhs=kbf[r][:, 0:n], start=True, stop=True),
             reads=['rperm', ('kbf', r)], writes=[('psS', r)])
        T.op('pool', lambda e: e.tensor_tensor(out=cs[r][:, 0:n], in0=kbf[r][:, 0:n], in1=cs[r][:, 0:n], op=ALU.mult),
             reads=[('kbf', r), ('cs', r)], writes=[('cs', r)])
        T.op('dve', lambda e: e.tensor_tensor(out=sn[r][:, 0:n], in0=psS[r][:, 0:n], in1=sn[r][:, 0:n], op=ALU.mult),
             reads=[('psS', r), ('sn', r)], writes=[('sn', r)])
        T.op('dve', lambda e: e.tensor_tensor(out=dst[:, dcol:dcol + n], in0=cs[r][:, 0:n], in1=sn[r][:, 0:n], op=ALU.add),
             reads=[('cs', r), ('sn', r)], writes=[dkey])

    def proj_seq(wt, wkey, items, moba, after=None):
        pend = None
        pidx = None
        for i, (c0, n, dst, dkey, dcol) in enumerate(items):
            pi = proj_mm(wt, wkey, c0, n)
            if pend is not None:
                rope_fin(*pend)
                if after is not None:
                    after(pidx)
                pend = None
            r = proj_fin(pi, c0, n, moba, dst, dkey, dcol)
            if moba:
                pend = (r, n, dst, dkey, dcol)
                pidx = i
            elif after is not None:
                after(i)
        if pend is not None:
            rope_fin(*pend)
            if after is not None:
                after(pidx)

    def head_weights(hidx):
        moba = hidx < 8
        h = hidx % 8
        base = 0 if moba else 3072
        load_w(wk[:], 'wk', wv(w_in)[:, :, base + 1024 + h * 128: base + 1024 + (h + 1) * 128], KC, 128)
        load_w(wvv[:], 'wvv', wv(w_in)[:, :, base + 2048 + h * 128: base + 2048 + (h + 1) * 128], KC, 128)
        load_w(wq[:], 'wq', wv(w_in)[:, :, base + h * 128: base + (h + 1) * 128], KC, 128)

    def gate_a_part(gi):
        c0, n = groups[gi]
        b0, nb = c0 // 256, n // 256
        T.op('dve', lambda e: e.tensor_reduce(out=ksum[:, b0:b0 + nb], in_=kT[:, c0:c0 + n].rearrange("p (b k) -> p b k", k=256), axis=AX.X, op=ALU.add),
             reads=[('kT', gi)], writes=['ksum'])

    def gate_a(hidx):
        T.op('dve', lambda e: e.tensor_copy(out=ksumb[:], in_=ksum[:]), reads=['ksum'], writes=['ksumb'])

    def gate_b(hidx):
        for s in range(NQS):
            T.op('pe', lambda e: e.matmul(psO[:, s * NBLK:(s + 1) * NBLK], lhsT=qT[:, s * 128:(s + 1) * 128], rhs=ksumb[:], start=True, stop=True),
                 reads=[('qT', s // 3), 'ksumb'], writes=['psO'])
        T.op('dve', lambda e: e.tensor_tensor(out=gm[:], in0=psO[:, 0:NQS * NBLK], in1=pastb[:], op=ALU.add), reads=['psO', 'pastb'], writes=['gm'])
        for s in range(NQS):
            T.op('dve', lambda e: e.max(out=m8[:, s * 8:(s + 1) * 8], in_=gm[:, s * NBLK:(s + 1) * NBLK]), reads=['gm'], writes=['m8'])
        for s in range(NQS):
            T.op('dve', lambda e: e.tensor_scalar(out=ge[:, s * NBLK:(s + 1) * NBLK], in0=gm[:, s * NBLK:(s + 1) * NBLK],
                                                  scalar1=m8[:, s * 8 + 2:s * 8 + 3], scalar2=None, op0=ALU.is_ge), reads=['gm', 'm8'], writes=['ge'])
        T.op('dve', lambda e: e.tensor_tensor(out=ge[:], in0=ge[:], in1=past01[:], op=ALU.mult), reads=['ge', 'past01'], writes=['ge'])
        T.op('dve', lambda e: e.tensor_tensor(out=ge[:], in0=ge[:], in1=own01[:], op=ALU.max), reads=['ge', 'own01'], writes=['ge'])
        T.op('dve', lambda e: e.tensor_scalar(out=maskv[:, :, 0:NBLK], in0=ge[:].rearrange("p (a b) -> p a b", b=NBLK), scalar1=-1.0, scalar2=-NEGM, op0=ALU.add, op1=ALU.mult), reads=['ge'], writes=['maskv'])

    def gate_c(hidx):
        for qt in range(3):
            pb = tr_i[0] % 2
            tr_i[0] += 1
            for i in range(3):
                s = qt * 3 + i
                T.op('pe', lambda e: e.transpose(out=psT[pb][0:32, i * 128:(i + 1) * 128], in_=maskv[:, s, :], identity=identb[:]),
                     reads=['maskv', 'identb'], writes=[('psT', pb)])
            T.op('dve', lambda e: e.tensor_copy(out=maskT[0:32, qt * QW:(qt + 1) * QW], in_=psT[pb][0:32, 0:QW]), reads=[('psT', pb)], writes=[('maskT', qt)])

    def fox_bias(hidx):
        h = hidx % 8
        for qt in range(3):
            T.op('dve', lambda e: e.tensor_scalar(out=fb[:, qt, :], in0=ckk[:, :, h], scalar1=-1.0, scalar2=cref[:, h * 3 + qt:h * 3 + qt + 1],
                                                  op0=ALU.mult, op1=ALU.add), reads=['ckk', 'cref'], writes=[('fb', qt)])

    sring = [(psS[0], ('psS', 0)), (psS[1], ('psS', 1)), (psA[0], ('psA', 0)), (psA[1], ('psA', 1))]
    acc = [(psO, 'psO', psD, 'psD'), (psTf[0], ('psT', 0), psTf[1], ('psT', 1))]
    sr_i = [0]
    acc_i = [0]
    pt_i = [0]
    LOOK = 2

    def attention(hidx):
        moba = hidx < 8
        items = []
        for qt in range(3):
            first_q = QT0 + 3 * qt
            full = [kt for kt in range(0, first_q)]
            part = [first_q + i for i in range(3)]
            order = full[0:6] + part + full[6:]
            a4 = acc[acc_i[0] % 2]
            acc_i[0] += 1
            for idx, kt in enumerate(order):
                c0 = (kt - first_q) * 128 if kt >= first_q else 0
                items.append(dict(qt=qt, kt=kt, c0=c0, diag=(kt >= first_q), first=(idx == 0), last=(idx == len(order) - 1), acc=a4))
        n = len(items)

        def qk(it):
            kt, c0, qt = it['kt'], it['c0'], it['qt']
            sb, sk = sring[sr_i[0] % 4]
            sr_i[0] += 1
            it['sb'] = (sb, sk)
            T.op('pe', lambda e: e.matmul(sb[:, c0:QW], lhsT=kT[:, kt * 128:(kt + 1) * 128], rhs=qT[:, qt * QW + c0:(qt + 1) * QW], start=True, stop=(not moba)),
                 reads=[('kT', kt // 4), ('qT', qt)], writes=[sk])
            if moba:
                blk = kt // 2
                T.op('pe', lambda e: e.matmul(sb[:, c0:QW], lhsT=esel[:, blk * 128:(blk + 1) * 128], rhs=maskT[:, qt * QW + c0:(qt + 1) * QW], start=False, stop=True),
                     reads=['esel', ('maskT', qt)], writes=[sk])

        def ex(it):
            kt, c0, qt = it['kt'], it['c0'], it['qt']
            sb, sk = it['sb']
            pi = pt_i[0] % 3
            pt_i[0] += 1
            pt = Pt[pi]
            pk = ('Pt', pi)
            it['pt'] = (pt, pk)
            if moba:
                T.op('act', lambda e: e.activation(out=pt[:, c0:QW], in_=sb[:, c0:QW], func=AF.Exp, scale=SCALE), reads=[sk], writes=[pk])
            else:
                T.op('act', lambda e: e.activation(out=pt[:, c0:QW], in_=sb[:, c0:QW], func=AF.Exp, scale=SCALE, bias=fb[:, qt, kt:kt + 1]),
                     reads=[sk, ('fb', qt)], writes=[pk])
            if it['diag']:
                T.op('dve', lambda e: e.tensor_tensor(out=pt[:, c0:c0 + 128], in0=pt[:, c0:c0 + 128], in1=tri[:], op=ALU.mult), reads=[pk, 'tri'], writes=[pk])

        for i in range(min(LOOK, n)):
            qk(items[i])
        ex(items[0])
        for i in range(n):
            it = items[i]
            if i + LOOK < n:
                qk(items[i + LOOK])
            if i + 1 < n:
                ex(items[i + 1])
            kt, c0, qt = it['kt'], it['c0'], it['qt']
            aO, kO, aD, kD = it['acc']
            pt, pk = it['pt']
            T.op('pe', lambda e: e.matmul(aO[:, c0:QW], lhsT=V[:, kt, :], rhs=pt[:, c0:QW], start=it['first'], stop=it['last']),
                 reads=[('V', kt // 4), pk], writes=[kO])
            T.op('pe', lambda e: e.matmul(aD[:, c0:QW], lhsT=onesb[:], rhs=pt[:, c0:QW], start=it['first'], stop=it['last']),
                 reads=['onesb', pk], writes=[kD])
            if it['last']:
                mi = (hidx * 3 + qt) % 2
                T.op('dve', lambda e: e.tensor_scalar(out=rec[mi][:], in0=aD[:, 0:QW], scalar1=1e-30, scalar2=None, op0=ALU.add), reads=[kD], writes=[('rec', mi)])
                T.op('dve', lambda e: e.reciprocal(out=rec[mi][:], in_=rec[mi][:]), reads=[('rec', mi)], writes=[('rec', mi)])
                T.op('dve', lambda e: e.tensor_tensor(out=mixo[mi][:], in0=aO[:, 0:QW], in1=rec[mi][:], op=ALU.mult), reads=[kO, ('rec', mi)], writes=[('mixo', mi)])
                T.dma(mixT_d[hidx, :, qt * QW:(qt + 1) * QW], mixo[mi][:], reads=[('mixo', mi)], writes=[('mixT_d', hidx, qt)])

    def v_proj(gsel):
        pend = None

        def fin(vb, c0, n, gi):
            pb = tr_i[0] % 2
            tr_i[0] += 1
            for s in range(n // 128):
                T.op('pe', lambda e: e.transpose(out=psT[pb][:, s * 128:(s + 1) * 128], in_=vtmp[vb][:, s * 128:(s + 1) * 128], identity=identb[:]),
                     reads=[('vtmp', vb), 'identb'], writes=[('psT', pb)])
            T.op('dve', lambda e: e.tensor_copy(out=V[:, c0 // 128:(c0 + n) // 128, :], in_=psT[pb][:, 0:n].rearrange("p (a b) -> p a b", b=128)),
                 reads=[('psT', pb)], writes=[('V', gi)])

        for gi in gsel:
            c0, n = groups[gi]
            pi = proj_mm(wvv, 'wvv', c0, n)
            if pend is not None:
                fin(*pend)
            vb = gi % 2
            T.op('act', lambda e: e.activation(out=vtmp[vb][:, 0:n], in_=psA[pi][:, 0:n], func=AF.Copy), reads=[('psA', pi)], writes=[('vtmp', vb)])
            pend = (vb, c0, n, gi)
        fin(*pend)

    groups = [(g * 512, 512) for g in range(8)] + [(4096, 256)]
    head_weights(0)
    n_heads = 16
    for hidx in range(n_heads):
        moba = hidx < 8
        proj_seq(wk, 'wk', [(c0, n, kT, ('kT', gi), c0) for gi, (c0, n) in enumerate(groups)], moba, after=(gate_a_part if moba else None))
        if moba:
            gate_a(hidx)
        else:
            fox_bias(hidx)
        proj_seq(wq, 'wq', [(QT0 * 128 + qt * QW, QW, qT, ('qT', qt), qt * QW) for qt in range(3)], moba)
        v_proj(range(0, 5))
        if moba:
            gate_b(hidx)
        v_proj(range(5, 8))
        if moba:
            gate_c(hidx)
        v_proj(range(8, 9))
        if DEBUG:
            T.dma(dbg['kT'][hidx], kT[:], reads=[('kT', g) for g in range(9)])
            T.dma(dbg['V'][hidx], V[:].rearrange("p a b -> p (a b)"), reads=[('V', g) for g in range(9)])
            T.dma(dbg['qT'][hidx], qT[:], reads=[('qT', g) for g in range(3)])
        if hidx + 1 < n_heads:
            head_weights(hidx + 1)
        attention(hidx)
    T.barrier()
    es.close()
    es_A.close()
    if DEBUG:
        T.dma(dbg['mixT'], mixT_d, reads=[])
        T.barrier()

    esF = ExitStack()
    actT = S(esF, "actT", [128, KC, NQ], BF16, side="right")
    xn3T = actT
    esB = ExitStack()
    hres = S(esB, "hres", [128, NQS, D], F32)
    junk = S(esB, "junk", [128, D], BF16)
    xnb = [S(esB, "xnb%d" % i, [128, D], BF16) for i in range(2)]
    grep = S(esB, "grep", [128, D], F32)

    K2T = S(esB, "K2T", [128, 4, 256], BF16)
    V2 = S(esB, "V2", [128, 2, 512], BF16)
    wgrp0 = S(esB, "wgrp0", [128, KC, 512], BF16)
    es = ExitStack()
    memT = S(es, "memT", [128, KC, 256], BF16)
    xt = [S(es, "xt0", [128, D], F32)]
    wckv = S(es, "wckv", [128, KC, 1024], BF16)
    load_g(g_mem)
    load_w(wckv[:], 'wckv', wv(w_ckv), KC, 1024, engs=('act',))
    load_w(wgrp0[:], ('wgrp', 0), wv(w_o)[:, :, 0:512], KC, 512, engs=('act',))

    def _ldm(mt):
        return lambda: T.dma(xt[0][:], memx[mt * 128:(mt + 1) * 128, :], writes=[('xt', 0)])
    norm_seq([(_ldm(mt), xt[0][:], ('xt', 0), 'memT', mt * 128) for mt in range(2)], memT)
    T.dma(actT[:], mixT_d.rearrange("h p n -> p h n"), writes=[('actT', s) for s in range(NQS)])
    for s in range(NQS):
        T.dma(hres[:, s, :], xc[(QT0 + s) * 128:(QT0 + s + 1) * 128, :], writes=[('hres', s)])
    for h in range(4):
        pi = pa_i[0] % 2
        pa_i[0] += 1
        for kc in range(KC):
            T.op('pe', lambda e: e.matmul(psA[pi][:, 0:256], lhsT=wckv[:, kc, h * 128:(h + 1) * 128], rhs=memT[:, kc, :], start=(kc == 0), stop=(kc == KC - 1)),
                 reads=['wckv', 'memT'], writes=[('psA', pi)])
        T.op('act', lambda e: e.activation(out=K2T[:, h, :], in_=psA[pi][:, 0:256], func=AF.Copy), reads=[('psA', pi)], writes=['K2T'])
    for mt in range(2):
        pi = pa_i[0] % 2
        pa_i[0] += 1
        for kc in range(KC):
            T.op('pe', lambda e: e.matmul(psA[pi][:, :], lhsT=memT[:, kc, mt * 128:(mt + 1) * 128], rhs=wckv[:, kc, 512:1024], start=(kc == 0), stop=(kc == KC - 1)),
                 reads=['wckv', 'memT'], writes=[('psA', pi)])
        T.op('act', lambda e: e.activation(out=V2[:, mt, :], in_=psA[pi][:, :], func=AF.Copy), reads=[('psA', pi)], writes=['V2'])
    T.barrier()
    es.close()

    es = ExitStack()
    wcq = S(es, "wcq", [128, KC, 512], BF16)
    wco = S(es, "wco", [128, 4, D], BF16)
    es2 = ExitStack()
    wgrp = [wgrp0, S(es2, "wgrp1", [128, KC, 512], BF16)]
    load_g(g_x)
    npipe = NormPipe(actT)
    for cg in range(4):
        wb = cg % 2
        if cg + 1 < 4:
            load_w(wgrp[1 - wb][:], ('wgrp', 1 - wb), wv(w_o)[:, :, (cg + 1) * 512:(cg + 2) * 512], KC, 512, engs=('pool', 'act'))
        if cg == 2:
            load_w(wcq[:], 'wcq', wv(w_cq), KC, 512, engs=('pool', 'act'))
            load_w(wco[:], 'wco', wv(w_co), 4, D, engs=('pool', 'act'))
        for s in range(NQS):
            pi = pa_i[0] % 2
            pa_i[0] += 1
            for kc in range(KC):
                T.op('pe', lambda e: e.matmul(psA[pi][:, :], lhsT=actT[:, kc, s * 128:(s + 1) * 128], rhs=wgrp[wb][:, kc, :], start=(kc == 0), stop=(kc == KC - 1)),
                     reads=[('actT', s), ('wgrp', wb)], writes=[('psA', pi)])
            T.op('dve', lambda e: e.tensor_tensor(out=hres[:, s, cg * 512:(cg + 1) * 512], in0=psA[pi][:, :], in1=hres[:, s, cg * 512:(cg + 1) * 512], op=ALU.add),
                 reads=[('psA', pi), ('hres', s)], writes=[('hres', s)])
            if cg == 3 and not DEBUG:
                npipe.push(hres[:, s, :], ('hres', s), ('actT', s), s * 128)
    if DEBUG:
        T.dma(dbg['h1'].rearrange("s p n -> p s n"), hres[:], reads=[('hres', s) for s in range(NQS)])
        for s in range(NQS):
            npipe.push(hres[:, s, :], ('hres', s), ('actT', s), s * 128)
    npipe.flush()
    T.barrier()
    es2.close()

    xn2T = actT
    es2 = ExitStack()
    q2T = S(es2, "q2T", [128, 4, NQ], BF16)
    o2T = S(es2, "o2T", [128, 4, NQ], BF16)
    Pt = [S(es2, "Pt%d" % i, [128, QW], BF16) for i in range(3)]
    rec = S(es2, "rec", [128, QW], F32)
    load_g(g_ffn)
    for h in range(4):
        for qt in range(3):
            pi = pa_i[0] % 2
            pa_i[0] += 1
            for kc in range(KC):
                T.op('pe', lambda e: e.matmul(psA[pi][:, 0:QW], lhsT=wcq[:, kc, h * 128:(h + 1) * 128], rhs=xn2T[:, kc, qt * QW:(qt + 1) * QW], start=(kc == 0), stop=(kc == KC - 1)),
                     reads=['wcq'] + [('actT', 3 * qt + i) for i in range(3)], writes=[('psA', pi)])
            T.op('act', lambda e: e.activation(out=q2T[:, h, qt * QW:(qt + 1) * QW], in_=psA[pi][:, 0:QW], func=AF.Copy), reads=[('psA', pi)], writes=[('q2T', h, qt)])
    pt_i = 0
    for h in range(4):
        for qt in range(3):
            for mt in range(2):
                T.op('pe', lambda e: e.matmul(psS[mt][:, 0:QW], lhsT=K2T[:, h, mt * 128:(mt + 1) * 128], rhs=q2T[:, h, qt * QW:(qt + 1) * QW], start=True, stop=True),
                     reads=['K2T', ('q2T', h, qt)], writes=[('psS', mt)])
            pts = []
            for mt in range(2):
                pt = Pt[pt_i % 3]
                pk = ('Pt', pt_i % 3)
                pt_i += 1
                pts.append((pt, pk))
                T.op('act', lambda e: e.activation(out=pt[:], in_=psS[mt][:, 0:QW], func=AF.Exp, scale=SCALE), reads=[('psS', mt)], writes=[pk])
            for mt in range(2):
                pt, pk = pts[mt]
                T.op('pe', lambda e: e.matmul(psO[:, 0:QW], lhsT=V2[:, mt, h * 128:(h + 1) * 128], rhs=pt[:], start=(mt == 0), stop=(mt == 1)), reads=['V2', pk], writes=['psO'])
                T.op('pe', lambda e: e.matmul(psD[:, 0:QW], lhsT=onesb[:], rhs=pt[:], start=(mt == 0), stop=(mt == 1)), reads=['onesb', pk], writes=['psD'])
            T.op('dve', lambda e: e.reciprocal(out=rec[:], in_=psD[:, 0:QW]), reads=['psD'], writes=['rec'])
            T.op('dve', lambda e: e.tensor_tensor(out=o2T[:, h, qt * QW:(qt + 1) * QW], in0=psO[:, 0:QW], in1=rec[:], op=ALU.mult), reads=['psO', 'rec'], writes=[('o2T', qt)])
    npipe = NormPipe(actT)

    def _after(s):
        def f():
            if s == 0:
                T.op('dve', lambda e: e.tensor_scalar(out=actT[:, :, 0:128], in0=actT[:, :, 0:128], scalar1=halov[:, 0:1], scalar2=None, op0=ALU.mult),
                     reads=[('actT', 0), 'halov'], writes=[('actT', 0)])
            T.dma(hres_d[s], hres[:, s, :], reads=[('hres', s)], writes=[('hd', s, cg) for cg in range(4)])
        return f

    for s in range(NQS):
        for cg in range(4):
            pi = pa_i[0] % 2
            pa_i[0] += 1
            for kc in range(4):
                T.op('pe', lambda e: e.matmul(psA[pi][:, :], lhsT=o2T[:, kc, s * 128:(s + 1) * 128], rhs=wco[:, kc, cg * 512:(cg + 1) * 512], start=(kc == 0), stop=(kc == 3)),
                     reads=[('o2T', s // 3), 'wco'], writes=[('psA', pi)])
            T.op('dve', lambda e: e.tensor_tensor(out=hres[:, s, cg * 512:(cg + 1) * 512], in0=psA[pi][:, :], in1=hres[:, s, cg * 512:(cg + 1) * 512], op=ALU.add),
                 reads=[('psA', pi), ('hres', s)], writes=[('hres', s)])
        if not DEBUG:
            npipe.push(hres[:, s, :], ('hres', s), ('actT', s), s * 128, after=_after(s))
    if DEBUG:
        T.dma(dbg['h2'].rearrange("s p n -> p s n"), hres[:], reads=[('hres', s) for s in range(NQS)])
        for s in range(NQS):
            npipe.push(hres[:, s, :], ('hres', s), ('actT', s), s * 128, after=_after(s))
    npipe.flush()
    T.barrier()
    es2.close()
    es.close()
    esB.close()

    AKEYS = [('actT', s_) for s_ in range(NQS)]
    es = ExitStack()
    aT = S(es, "aT", [128, NFC, 1024], BF16)
    wd0 = S(es, "wd0", [128, NFC, 512], BF16)
    es2 = ExitStack()
    wgu = [S(es2, "wgu%d" % i, [128, KC, 256], BF16) for i in range(2)]
    convw = S(es2, "convw", [128, 88, 3], F32)
    convb = S(es2, "convb", [128, 88], F32)
    yg = [S(es2, "yg%d" % i, [128, 344], F32) for i in range(2)]
    yu = [S(es2, "yu%d" % i, [128, 344], F32) for i in range(2)]
    sg = [S(es2, "sg%d" % i, [128, 344], F32) for i in range(2)]
    T.dma(convw[:].rearrange("p a b -> p (a b)"), convw_d, writes=['convw'])
    T.dma(convb[:], convb_d, writes=['convb'])
    it = 0
    def load_gu(c):
        wb = c % 2
        load_w(wgu[wb][:, :, 0:128], ('wgu', wb), wv(w_up)[:, :, c * 128:(c + 1) * 128], KC, 128, engs=('pool',))
        load_w(wgu[wb][:, :, 128:256], ('wgu', wb), wv(w_up)[:, :, DFF + c * 128:DFF + (c + 1) * 128], KC, 128, engs=('act',))

    load_gu(0)
    for c in range(NFC):
        wb = c % 2
        for tt in range(3):
            o0 = 342 * tt
            o1 = min(1024, 342 * (tt + 1))
            no = o1 - o0
            n = no + 2
            c0 = 128 + o0 - 2
            pg, pu = (psA[0], psA[1]) if it % 2 == 0 else (psS[0], psS[1])
            kg, ku = (('psA', 0), ('psA', 1)) if it % 2 == 0 else (('psS', 0), ('psS', 1))
            b = it % 2
            it += 1
            for kc in range(KC):
                T.op('pe', lambda e: e.matmul(pg[:, 0:n], lhsT=wgu[wb][:, kc, 0:128], rhs=xn3T[:, kc, c0:c0 + n], start=(kc == 0), stop=(kc == KC - 1)),
                     reads=[('wgu', wb)] + AKEYS, writes=[kg])
            for kc in range(KC):
                T.op('pe', lambda e: e.matmul(pu[:, 0:n], lhsT=wgu[wb][:, kc, 128:256], rhs=xn3T[:, kc, c0:c0 + n], start=(kc == 0), stop=(kc == KC - 1)),
                     reads=[('wgu', wb)] + AKEYS, writes=[ku])
            for (pp, pkey, yy, ykey, ch) in [(pg, kg, yg[b], ('yg', b), c), (pu, ku, yu[b], ('yu', b), NFC + c)]:
                T.op('act', lambda e: e.activation(out=yy[:, 0:no], in_=pp[:, 2:n], func=AF.Identity, scale=convw[:, ch, 2:3], bias=convb[:, ch:ch + 1]),
                     reads=[pkey, 'convw', 'convb'], writes=[ykey])
                T.op('dve', lambda e: e.scalar_tensor_tensor(out=yy[:, 0:no], in0=pp[:, 1:n - 1], scalar=convw[:, ch, 1:2], in1=yy[:, 0:no], op0=ALU.mult, op1=ALU.add),
                     reads=[pkey, 'convw', ykey], writes=[ykey])
                T.op('dve', lambda e: e.scalar_tensor_tensor(out=yy[:, 0:no], in0=pp[:, 0:n - 2], scalar=convw[:, ch, 0:1], in1=yy[:, 0:no], op0=ALU.mult, op1=ALU.add),
                     reads=[pkey, 'convw', ykey], writes=[ykey])
            T.op('act', lambda e: e.activation(out=sg[b][:, 0:no], in_=yg[b][:, 0:no], func=AF.Silu), reads=[('yg', b)], writes=[('sg', b)])
            T.op('dve', lambda e: e.tensor_tensor(out=aT[:, c, o0:o1], in0=sg[b][:, 0:no], in1=yu[b][:, 0:no], op=ALU.mult),
                 reads=[('sg', b), ('yu', b)], writes=[('aT', c)])
            if tt == 0 and c + 1 < NFC:
                load_gu(c + 1)
            if tt == 1 and NFC - 14 <= c < NFC - 3:
                k_ = c - (NFC - 14)
                load_w(wd0[:, 4 * k_:4 * k_ + 4, :], ('wd', 0), wv(w_down)[:, 4 * k_:4 * k_ + 4, 0:512], 4, 512, engs=('pool',))
    T.barrier()
    es2.close()
    esF.close()

    es2 = ExitStack()
    wd = [wd0, S(es2, "wd1", [128, NFC, 512], BF16)]
    h2t = [S(es2, "h2t%d" % i, [128, 512], F32) for i in range(8)]
    for cg in range(4):
        wb = cg % 2
        for s in range(8):
            T.dma(h2t[s][:], hres_d[1 + s, :, cg * 512:(cg + 1) * 512], reads=[('hd', 1 + s, cg)], writes=[('h2t', s)])
        if cg + 1 < 4:
            load_w(wd[1 - wb][:], ('wd', 1 - wb), wv(w_down)[:, :, (cg + 1) * 512:(cg + 2) * 512], NFC, 512, engs=('act', 'act', 'pool'))
        for s in range(8):
            pi = pa_i[0] % 2
            pa_i[0] += 1
            for fc in range(NFC):
                T.op('pe', lambda e: e.matmul(psA[pi][:, :], lhsT=aT[:, fc, s * 128:(s + 1) * 128], rhs=wd[wb][:, fc, :], start=(fc == 0), stop=(fc == NFC - 1)),
                     reads=[('aT', fc), ('wd', wb)], writes=[('psA', pi)])
            T.op('dve', lambda e: e.tensor_tensor(out=h2t[s][:], in0=psA[pi][:, :], in1=h2t[s][:], op=ALU.add), reads=[('psA', pi), ('h2t', s)], writes=[('h2t', s)])
            T.dma(hres_d[1 + s, :, cg * 512:(cg + 1) * 512], h2t[s][:], reads=[('h2t', s)], writes=[('hd', 1 + s, cg)])
    T.barrier()
    es2.close()
    es.close()
    if DEBUG:
        T.dma(dbg['h3'], hres_d, reads=[])
        T.barrier()

    es = ExitStack()
    xt = [S(es, "xt%d" % i, [128, D], F32) for i in range(3)]
    yo = [S(es, "yo%d" % i, [128, D], F32) for i in range(2)]
    junk = S(es, "junk", [128, D], BF16)
    grep = S(es, "grep", [128, D], F32)
    load_g(g_fin)

    def _ldf(s_):
        T.dma(xt[s_ % 3][:], hres_d[1 + s_], reads=[('hd', 1 + s_, cg) for cg in range(4)], writes=[('xt', s_ % 3)])
    _ldf(0)
    _ldf(1)
    for s in range(8):
        b = s % 2
        xb_ = s % 3
        rs, sk = rms_stats(xt[xb_][:], ('xt', xb_))
        T.op('dve', lambda e: e.scalar_tensor_tensor(out=yo[b][:], in0=xt[xb_][:], scalar=rs, in1=grep[:], op0=ALU.mult, op1=ALU.mult),
             reads=[('xt', xb_), sk, 'grep'], writes=[('yo', b)])
        if s + 2 < 8:
            _ldf(s + 2)
        T.dma(y[s * 128:(s + 1) * 128, :], yo[b][:], reads=[('yo', b)])
    T.barrier(engines=('sp',))
    T.barrier(engines=('pe', 'act', 'dve', 'pool'))
    es.close()
    es_top.close()
    return nc, T


_CACHE = {}


def _rope_tables(pos):
    half = 64
    inv = (np.float32(10000.0) ** (-(np.arange(half, dtype=np.float32)) / np.float32(half))).astype(np.float32)
    ang = (pos.astype(np.float32)[None, :] * inv[:, None]).astype(np.float32)
    c = np.cos(ang).astype(np.float32)
    s = np.sin(ang).astype(np.float32)
    cosT = np.concatenate([c, c], axis=0)
    sinT = np.concatenate([-s, s], axis=0)
    return np.ascontiguousarray(cosT), np.ascontiguousarray(sinT)


def _core_inputs(j, xb, memb, shared):
    others = [i for i in range(4) if i != j]
    xo = [xb[i * 1024:(i + 1) * 1024] for i in others]
    if j > 0:
        halo = xb[j * 1024 - 256:j * 1024]
        halo_pos = np.arange(j * 1024 - 256, j * 1024)
    else:
        halo = np.zeros((256, D), np.float32)
        halo_pos = np.arange(256)
    own = xb[j * 1024:(j + 1) * 1024]
    xcx = np.ascontiguousarray(np.concatenate(xo + [halo, own], axis=0))
    pos = np.concatenate([np.arange(i * 1024, (i + 1) * 1024) for i in others] + [halo_pos, np.arange(j * 1024, (j + 1) * 1024)])
    cosT, sinT = _rope_tables(pos)
    valid = np.zeros(NCT, np.float32)
    for kt in range(24):
        i = others[kt // 8]
        valid[kt] = 1.0 if (i < j and not (i == j - 1 and kt % 8 >= 6)) else 0.0
    valid[24:26] = 1.0 if j > 0 else 0.0
    valid[26:] = 1.0
    kbias = np.tile(((1.0 - valid) * NEGM)[None, :], (128, 1)).astype(np.float32)
    lfvalid = np.tile(valid[None, :], (8, 1)).astype(np.float32)
    halov = np.full((128, 1), 1.0 if j > 0 else 0.0, np.float32)
    past = np.zeros((NQS, NBLK), np.float32)
    ownm = np.zeros((NQS, NBLK), np.float32)
    for s in range(NQS):
        tb = (QT0 + s) // 2
        ownm[s, tb] = 1.0
        for b in range(NBLK):
            if b < 12:
                i = others[b // 4]
                ok = (i < j) and not (i == j - 1 and b % 4 == 3)
            elif b == 12:
                ok = (j > 0) and tb > 12
            else:
                ok = b < tb
            past[s, b] = 1.0 if ok else 0.0
    pastbias = ((1.0 - past) * -1e30).astype(np.float32)
    rep = lambda a: np.ascontiguousarray(np.tile(a.reshape(1, -1), (128, 1)).astype(np.float32))
    m = dict(shared)
    m.update({"xc": xcx, "memx": np.ascontiguousarray(memb), "cosT": cosT, "sinT": sinT, "kbias": kbias, "lfvalid": lfvalid,
              "halov": halov, "pastbias": rep(pastbias), "past01": rep(past), "own01": rep(ownm)})
    return m


def kernel(x, mem, attn_norm_g, w_in, b_f, w_o, xattn_norm_g, mem_norm_g, w_cq, w_ckv, w_co, ffn_norm_g,
           w_up, conv_w, conv_b, w_down, final_norm_g):
    f = lambda a: np.ascontiguousarray(np.asarray(a, dtype=np.float32))
    x = f(x); mem = f(mem)
    esel = np.zeros((128, NBLK, 128), np.float32)
    for b in range(NBLK):
        esel[b, b, :] = 1.0
    rp = np.zeros((128, 128), np.float32)
    for m_ in range(128):
        rp[(m_ + 64) % 128, m_] = 1.0
    tri = (np.arange(128)[:, None] <= np.arange(128)[None, :]).astype(np.float32)
    cw = f(conv_w)[0]
    shared = {
        "g_attn": f(attn_norm_g).reshape(1, D), "g_x": f(xattn_norm_g).reshape(1, D), "g_mem": f(mem_norm_g).reshape(1, D),
        "g_ffn": f(ffn_norm_g).reshape(1, D), "g_fin": f(final_norm_g).reshape(1, D),
        "w_in": f(w_in)[0], "w_o": f(w_o)[0], "w_cq": f(w_cq)[0], "w_ckv": f(w_ckv)[0], "w_co": f(w_co)[0],
        "w_up": f(w_up)[0], "w_down": f(w_down)[0],
        "convw": np.ascontiguousarray(cw.reshape(3, 88, 128).transpose(2, 1, 0).reshape(128, 88 * 3)),
        "convb": np.ascontiguousarray(f(conv_b)[0].reshape(88, 128).T),
        "bf": f(b_f).reshape(8, 1),
        "identb": np.eye(128, dtype=np.float32).astype(NPBF), "rperm": rp.astype(NPBF), "onesb": np.ones((128, 128), NPBF),
        "tri": tri.astype(NPBF), "esel": esel.reshape(128, NBLK * 128).astype(NPBF),
        "ident8": np.eye(8, dtype=np.float32), "ones8": np.ones((8, 128), np.float32),
    }
    if 'nc' not in _CACHE:
        _CACHE['nc'] = build_program()
    nc, T = _CACHE['nc']
    in_maps = []
    for c in range(8):
        b, j = c // 4, c % 4
        in_maps.append(_core_inputs(j, x[b], mem[b], shared))
    res = run_bass_kernel_spmd(nc, in_maps, core_ids=list(range(8)))
    _CACHE['last'] = res
    out = np.zeros((2, 4096, D), np.float32)
    for c in range(8):
        b, j = c // 4, c % 4
        out[b, j * 1024:(j + 1) * 1024] = res.results[c]["y"]
    return out
```
